# Optimizing a Trainium2 kernel written in Bass

```python
import math
import jax, jax.numpy as jnp
from jax import lax
import numpy as np

D_MODEL = 2048
BATCH = 4
SEQ = 8192
DEPTH = 1

EPS = 1e-6
GM_HEADS = 8
GM_HEAD_DIM = 128
GM_W = GM_HEADS * GM_HEAD_DIM
GM_CHUNK = 128
ML_HEADS = 4
ML_QK_DIM = 128
ML_V_DIM = 256
ML_QK_W = ML_HEADS * ML_QK_DIM
ML_W = ML_HEADS * ML_V_DIM
ML_CHUNK = 128
ML_CONV = 4
MIX_W = GM_W + ML_W
SPLITS = [GM_W, GM_W, ML_QK_W, ML_QK_W, ML_W, ML_W, ML_HEADS, ML_HEADS]
PROJ_W = sum(SPLITS)
PEER_HEADS = 8
PEER_QDIM = 128
PEER_HALF = PEER_QDIM // 2
N_KEYS = 128
N_EXPERTS = N_KEYS * N_KEYS
PEER_TOPK = 16
PEER_TOKENS = 128

kernel_name = "hymba_gmlp_mlstm_peer_layer"


def rmsnorm(x, g):
    xf = x.astype(jnp.float32)
    y = xf * lax.rsqrt(jnp.mean(xf * xf, axis=-1, keepdims=True) + EPS)
    return (y * g.astype(jnp.float32)).astype(x.dtype)


def head_rmsnorm(y, g, n_heads):
    shp = y.shape
    yh = y.reshape(shp[:-1] + (n_heads, shp[-1] // n_heads))
    return rmsnorm(yh, g.reshape(n_heads, -1)).reshape(shp)


def causal_depthwise_conv(x, w, b):
    c = x.shape[-1]
    y = lax.conv_general_dilated(x, w[:, None, :].astype(x.dtype), window_strides=(1,),
                                 padding=[(w.shape[0] - 1, 0)],
                                 dimension_numbers=("NWC", "WIO", "NWC"),
                                 feature_group_count=c)
    return y + b.astype(x.dtype)


def mlstm_chunk_step(carry, inp):
    C, n, m = carry
    q, k, v, ig, lf = inp
    L = q.shape[2]
    causal = jnp.tril(jnp.ones((L, L), dtype=bool))
    b = jnp.cumsum(lf, axis=-1)
    dmat = b[..., :, None] - b[..., None, :] + ig[..., None, :]
    dmat = jnp.where(causal, dmat, -jnp.inf)
    inter = b + m[..., None]
    m_t = jnp.maximum(inter, jnp.max(dmat, axis=-1))
    w_intra = jnp.exp(dmat - m_t[..., None])
    a_inter = jnp.exp(inter - m_t)
    s = jnp.einsum("bhtd,bhsd->bhts", q, k) * w_intra
    num = jnp.einsum("bhts,bhsv->bhtv", s, v) + a_inter[..., None] * jnp.einsum("bhtd,bhdv->bhtv", q, C)
    den = jnp.sum(s, axis=-1) + a_inter * jnp.einsum("bhtd,bhd->bht", q, n)
    h = num / jnp.maximum(jnp.abs(den), jnp.exp(-m_t))[..., None]
    b_end = b[..., -1]
    g = b_end[..., None] - b + ig
    m_new = jnp.maximum(b_end + m, jnp.max(g, axis=-1))
    ws = jnp.exp(g - m_new[..., None])
    ac = jnp.exp(b_end + m - m_new)
    C_new = ac[..., None, None] * C + jnp.einsum("bhs,bhsd,bhsv->bhdv", ws, k, v)
    n_new = ac[..., None] * n + jnp.einsum("bhs,bhsd->bhd", ws, k)
    return (C_new, n_new, m_new), h


def setup_inputs(seed: int = 0) -> dict:
    key = jax.random.key(seed)
    ks = jax.random.split(key, 24)
    f32 = jnp.float32
    nrm = lambda k, shp, s: jax.random.normal(k, shp, f32) * s
    gain = lambda k, n: 1.0 + 0.02 * jax.random.normal(k, (n,), f32)
    tri = jnp.tril(jnp.ones((GM_CHUNK, GM_CHUNK), f32))
    return {
        "x": jax.random.normal(ks[0], (BATCH, SEQ, D_MODEL), f32),
        "norm1_g": gain(ks[1], D_MODEL),
        "w_in": nrm(ks[2], (D_MODEL, PROJ_W), D_MODEL ** -0.5),
        "gm_vnorm_g": gain(ks[3], GM_W),
        "w_spatial": nrm(ks[4], (GM_HEADS, GM_CHUNK, GM_CHUNK), 0.5 * GM_CHUNK ** -0.5) * tri,
        "b_spatial": 1.0 + 0.01 * jax.random.normal(ks[5], (GM_HEADS, GM_CHUNK), f32),
        "ml_conv_w": nrm(ks[6], (ML_CONV, 2 * ML_QK_W), ML_CONV ** -0.5),
        "ml_conv_b": nrm(ks[7], (2 * ML_QK_W,), 0.01),
        "ml_b_i": nrm(ks[8], (ML_HEADS,), 0.1),
        "ml_b_f": jnp.linspace(3.0, 6.0, ML_HEADS, dtype=f32) + nrm(ks[9], (ML_HEADS,), 0.1),
        "gm_out_g": gain(ks[10], GM_W),
        "ml_out_g": gain(ks[11], ML_W),
        "w_out": nrm(ks[12], (MIX_W, D_MODEL), MIX_W ** -0.5),
        "norm2_g": gain(ks[13], D_MODEL),
        "peer_wq": nrm(ks[14], (D_MODEL, PEER_HEADS * PEER_QDIM), D_MODEL ** -0.5),
        "peer_k1": nrm(ks[15], (PEER_HEADS, N_KEYS, PEER_HALF), PEER_HALF ** -0.5),
        "peer_k2": nrm(ks[16], (PEER_HEADS, N_KEYS, PEER_HALF), PEER_HALF ** -0.5),
        "peer_u": nrm(ks[17], (N_EXPERTS, D_MODEL), D_MODEL ** -0.5),
        "peer_v": nrm(ks[18], (N_EXPERTS, D_MODEL), (PEER_HEADS * PEER_TOPK) ** -0.5),
        "final_g": gain(ks[19], D_MODEL),
    }


def reference(x, norm1_g, w_in, gm_vnorm_g, w_spatial, b_spatial, ml_conv_w, ml_conv_b,
              ml_b_i, ml_b_f, gm_out_g, ml_out_g, w_out, norm2_g, peer_wq, peer_k1,
              peer_k2, peer_u, peer_v, final_g):
    B, S, D = x.shape
    f32 = jnp.float32
    for _layer in range(DEPTH):
        h = rmsnorm(x, norm1_g)
        proj = h @ w_in.astype(h.dtype)
        idx = np.cumsum(SPLITS)[:-1].tolist()
        p_u, p_v, p_q, p_k, p_mv, p_o, p_i, p_f = jnp.split(proj, idx, axis=-1)

        nc_g = S // GM_CHUNK
        u = jax.nn.gelu(p_u)
        vg = rmsnorm(jax.nn.gelu(p_v), gm_vnorm_g)
        vc = vg.reshape(B, nc_g, GM_CHUNK, GM_HEADS, GM_HEAD_DIM)
        causal = jnp.tril(jnp.ones((GM_CHUNK, GM_CHUNK), dtype=bool))
        ws_m = jnp.where(causal, w_spatial, 0.0).astype(vc.dtype)
        mixed = jnp.einsum("hts,bcshd->bcthd", ws_m, vc) + b_spatial.T.astype(vc.dtype)[None, None, :, :, None]
        y_gm = u * mixed.reshape(B, S, GM_W)

        qk = jax.nn.silu(causal_depthwise_conv(jnp.concatenate([p_q, p_k], axis=-1), ml_conv_w, ml_conv_b))
        q, k = jnp.split(qk.astype(f32), 2, axis=-1)
        q = q.reshape(B, S, ML_HEADS, ML_QK_DIM)
        k = k.reshape(B, S, ML_HEADS, ML_QK_DIM) * (ML_QK_DIM ** -0.5)
        v = p_mv.astype(f32).reshape(B, S, ML_HEADS, ML_V_DIM)
        ig = p_i.astype(f32) + ml_b_i.astype(f32)
        lf = jax.nn.log_sigmoid(p_f.astype(f32) + ml_b_f.astype(f32))
        nc_m = S // ML_CHUNK
        to_c = lambda a: jnp.transpose(a.reshape(B, nc_m, ML_CHUNK, ML_HEADS, a.shape[-1]), (1, 0, 3, 2, 4))
        to_cg = lambda a: jnp.transpose(a.reshape(B, nc_m, ML_CHUNK, ML_HEADS), (1, 0, 3, 2))
        init = (jnp.zeros((B, ML_HEADS, ML_QK_DIM, ML_V_DIM), f32),
                jnp.zeros((B, ML_HEADS, ML_QK_DIM), f32),
                jnp.zeros((B, ML_HEADS), f32))
        _, hs = lax.scan(mlstm_chunk_step, init, (to_c(q), to_c(k), to_c(v), to_cg(ig), to_cg(lf)))
        h_ml = jnp.transpose(hs, (1, 0, 3, 2, 4)).reshape(B, S, ML_W).astype(x.dtype)
        y_ml = jax.nn.sigmoid(p_o) * h_ml

        y_mix = jnp.concatenate([head_rmsnorm(y_gm, gm_out_g, GM_HEADS),
                                 head_rmsnorm(y_ml, ml_out_g, ML_HEADS)], axis=-1)
        x = x + y_mix @ w_out.astype(y_mix.dtype)

        h2 = rmsnorm(x, norm2_g)
        T = B * S
        hb = h2.reshape(T // PEER_TOKENS, PEER_TOKENS, D)

        def peer_block(xc):
            qp = (xc @ peer_wq.astype(xc.dtype)).reshape(PEER_TOKENS, PEER_HEADS, PEER_QDIM)
            q1, q2 = qp[..., :PEER_HALF], qp[..., PEER_HALF:]
            s1 = jnp.einsum("thd,hnd->thn", q1, peer_k1.astype(xc.dtype)).astype(f32)
            s2 = jnp.einsum("thd,hnd->thn", q2, peer_k2.astype(xc.dtype)).astype(f32)
            v1, i1 = lax.top_k(s1, PEER_TOPK)
            v2, i2 = lax.top_k(s2, PEER_TOPK)
            cand_s = (v1[..., :, None] + v2[..., None, :]).reshape(PEER_TOKENS, PEER_HEADS, PEER_TOPK * PEER_TOPK)
            cand_i = (i1[..., :, None] * N_KEYS + i2[..., None, :]).reshape(PEER_TOKENS, PEER_HEADS, PEER_TOPK * PEER_TOPK)
            sv, pos = lax.top_k(cand_s, PEER_TOPK)
            eidx = jnp.take_along_axis(cand_i, pos, axis=-1)
            gate = jax.nn.softmax(sv, axis=-1).astype(xc.dtype)
            u_sel = peer_u.astype(xc.dtype)[eidx]
            v_sel = peer_v.astype(xc.dtype)[eidx]
            act = jax.nn.gelu(jnp.einsum("td,thkd->thk", xc, u_sel))
            return jnp.einsum("thk,thkd->td", gate * act, v_sel)

        y_peer = lax.map(peer_block, hb).reshape(B, S, D)
        x = x + y_peer
    return rmsnorm(x, final_g)
```

```python
import numpy as np
from contextlib import ExitStack
import concourse.bass as bass
import concourse.mybir as mybir
from concourse.bass_utils import run_bass_kernel_spmd

F32 = mybir.dt.float32
BF16 = mybir.dt.bfloat16
AF = mybir.ActivationFunctionType
ALU = mybir.AluOpType
AX = mybir.AxisListType

P = 128
D = 2048
KD = 16
TB = 512
NT = 4
PROJ_W = 5128
EPS = 1e-6
NCORES = 8
SEQ = 8192
BATCH = 4
NEXP = 16384
DVEH_CFG = (1, 2, 3)
GC = 4
NG = 128 // GC
GE = GC * 128
NEG = -1.0e30


class Sched:
    ENGS = ("pe", "act", "dve", "pool", "sp")

    def __init__(self, nc):
        self.nc = nc
        self.ops = {e: [] for e in self.ENGS}
        self.dcnt = {}
        self.lastw = {}
        self.readers = {}
        self.waited = {}
        self.same_engine_sync = True
        self.regions = {}
        self.keyreg = {}

    def alias(self, key, region, lo, hi):
        self.regions.setdefault(region, []).append((key, lo, hi))
        self.keyreg[key] = (region, lo, hi)

    def op(self, eng, fn, r=(), w=(), dma=None):
        deps = []
        for k in r:
            t = self.lastw.get(k)
            if t is not None:
                deps.append(t)
        for k in w:
            t = self.lastw.get(k)
            if t is not None:
                deps.append(t)
            deps.extend(self.readers.get(k, ()))
            if k in self.keyreg:
                reg, lo, hi = self.keyreg[k]
                for (k2, lo2, hi2) in self.regions[reg]:
                    if k2 != k and lo < hi2 and lo2 < hi:
                        t = self.lastw.get(k2)
                        if t is not None:
                            deps.append(t)
                        deps.extend(self.readers.get(k2, ()))
        waits = {}
        for (s, v, e) in deps:
            if e == eng and (eng == "pe" or not self.same_engine_sync):
                continue
            if self.waited.get((eng, s), 0) >= v:
                continue
            if waits.get(s, 0) < v:
                waits[s] = v
        for s, v in waits.items():
            self.waited[(eng, s)] = v
        if dma is None:
            s = "E_" + eng
            self.dcnt[s] = self.dcnt.get(s, 0) + 1
            v = self.dcnt[s]
            tok = (s, v, eng)
            self.ops[eng].append([fn, waits, None, v])
        else:
            s = "D_" + dma
            self.dcnt[s] = self.dcnt.get(s, 0) + 16
            v = self.dcnt[s]
            tok = (s, v, "dma")
            self.ops[eng].append([fn, waits, s, v])
        for k in r:
            self.readers.setdefault(k, []).append(tok)
        for k in w:
            self.lastw[k] = tok
            self.readers[k] = []
        return tok

    def emit(self, final_waits):
        nc = self.nc
        needed = {e: set() for e in self.ENGS}
        for e in self.ENGS:
            for (fn, waits, ds, v) in self.ops[e]:
                for s, wv in waits.items():
                    if s.startswith("E_"):
                        needed[s[2:]].add(wv)
        rank = {}
        for e in self.ENGS:
            for i, v in enumerate(sorted(needed[e])):
                rank[(e, v)] = i + 1
        names = set()
        for e in self.ENGS:
            if needed[e]:
                names.add("E_" + e)
            for (fn, waits, ds, v) in self.ops[e]:
                if ds is not None:
                    names.add(ds)
        with ExitStack() as st:
            sems = {n: st.enter_context(nc.semaphore(n)) for n in sorted(names)}
            block = st.enter_context(nc.Block())

            def run(eng_name):
                def body(e):
                    for (fn, waits, ds, v) in self.ops[eng_name]:
                        for s, wv in waits.items():
                            if s.startswith("E_"):
                                e.wait_ge(sems[s], rank[(s[2:], wv)])
                            else:
                                e.wait_ge(sems[s], wv)
                        ins = fn(e)
                        if ds is not None:
                            ins.then_inc(sems[ds], 16)
                        elif v in needed[eng_name]:
                            ins.then_inc(sems["E_" + eng_name], 1)
                    if eng_name == "sp":
                        for s in final_waits:
                            e.wait_ge(sems["D_" + s], self.dcnt["D_" + s])
                return body

            block.tensor(run("pe"))
            block.scalar(run("act"))
            block.vector(run("dve"))
            block.gpsimd(run("pool"))
            block.sync(run("sp"))


def build(nb_pre, nb_own, dbg=(), stop_after=None):
    nc = bass.Bass("TRN2", target_bir_lowering=False)
    ntok = nb_own * TB
    npre = max(nb_pre, 1) * TB

    def din(name, shape):
        return nc.dram_tensor(name, list(shape), F32, kind="ExternalInput").ap()

    xo = din("xo", [ntok, D])
    xp = din("xp", [npre, D])
    flag_d = din("flag", [P, 1])
    g1T_d = din("g1T", [P, KD])
    g2T_d = din("g2T", [P, KD])
    gmixT_d = din("gmixT", [P, KD])
    gf_d = din("gf", [D])
    gvn_d = din("gvn", [1024])
    w_in_d = din("w_in", [D, PROJ_W])
    wsT_d = din("wsT", [P, 8, P])
    bsp_d = din("bsp", [P, 8])
    cw_d = din("cw", [P, 8, 4])
    cb_d = din("cb", [P, 8])
    bi_d = din("bi", [4])
    bf_d = din("bf", [4])
    w_out_d = din("w_out", [D, D])
    wq_d = din("wq", [D, 1024])
    kT12_d = din("kT12", [P, 8, 256])
    UT_d = din("UT", [P, KD, NEXP])
    V_d = din("V", [NEXP, D])
    ident_d = din("ident", [P, P])
    maskT_d = din("maskT", [P, P])
    y_d = nc.dram_tensor("y", [ntok, D], F32, kind="ExternalOutput").ap()
    dbg_out = {}

    w_in_v = w_in_d.rearrange("(k p) n -> p k n", p=P)
    w_out_v = w_out_d.rearrange("(k p) n -> p k n", p=P)
    wq_v = wq_d.rearrange("(k p) n -> p k n", p=P)
    V_v = V_d.rearrange("(c p) d -> p c d", p=P)

    S = Sched(nc)
    es = ExitStack()

    def sb(name, shape, dt):
        return es.enter_context(nc.sbuf_tensor("s_" + name, list(shape), dt))

    def ps(name, shape, dt=F32):
        return es.enter_context(nc.psum_tensor("p_" + name, list(shape), dt))

    xres = sb("xres", [P, NT, D], F32)
    Hreg = sb("Hreg", [P, KD, TB], BF16)
    Wring = [sb(f"W{i}", [P, KD, 512], BF16) for i in range(3)]
    Yreg = sb("Yreg", [P, NT * D], BF16)
    Greg = sb("Greg", [P, 12288], BF16)
    ident = sb("ident", [P, P], BF16)
    maskT = sb("maskT", [P, P], BF16)
    tri32 = sb("tri32", [P, P], F32)
    ones32 = sb("ones32", [P, P], F32)
    g1T = sb("g1T", [P, KD], F32)
    g2T = sb("g2T", [P, KD], F32)
    gmixT = sb("gmixT", [P, KD], F32)
    gvn_rep = sb("gvn_rep", [P, 1024], F32)
    wsT = sb("wsT", [P, 8, P], BF16)
    bsp = sb("bsp", [P, 8], F32)
    cw = sb("cw", [P, 8, 4], F32)
    cb = sb("cb", [P, 8], F32)
    bi_rep = sb("bi_rep", [P, 4], F32)
    bf_rep = sb("bf_rep", [P, 4], F32)
    kT12 = sb("kT12", [P, 8, 256], BF16)
    flag = sb("flag", [P, 1], F32)
    Cst = sb("Cst", [P, 4, 257], F32)
    hist = sb("hist", [P, 8, 3], F32)
    xn = [sb(f"xn{i}", [P, D], BF16) for i in range(2)]
    junk = xn[1]
    ctmp = [sb(f"ctmp{i}", [P, 512], F32) for i in range(2)]
    sg = [sb(f"sg{i}", [P, 512], F32) for i in range(2)]
    qT = sb("qT", [P, 4, TB], BF16)
    kT = sb("kT", [P, 4, TB], BF16)
    ktok = sb("ktok", [P, NT, 4, P], BF16)
    tmpf = sb("tmpf", [P, 1024], F32)
    sTm = sb("sTm", [P, 4, P], BF16)
    Cs_bf = sb("Cs_bf", [P, 4, 257], BF16)
    Cs_f = sb("Cs_f", [P, 4, 257], F32)
    ssq = sb("ssq", [P, 16], F32)
    rstd = sb("rstd", [P, 16], F32)
    if_sb = sb("if_sb", [P, NT, 8], F32)
    gsm = sb("gsm", [P, NT, 32], F32)
    hsm = sb("hsm", [P, 32], F32)

    tv = sb("tv", [P, 32], F32)
    svall = sb("svall", [P, 8, 24], F32)
    psc = sb("psc", [P, NT, 8, 3], F32)
    pst = sb("pst", [P, 8, 24], F32)
    S12 = []
    for i in range(NT):
        base = Yreg[:, i * 4096:(i + 1) * 4096] if i < 2 else Greg[:, (i - 2) * 4096:(i - 1) * 4096]
        S12.append(base.bitcast(F32).rearrange("p (h c) -> p h c", h=8))
    qTp = Greg[:, 8192:12288].rearrange("p (h c) -> p h c", h=8)
    zb = ctmp
    Eb = [sg[j][:].bitcast(BF16)[:, 0:512] for j in range(2)]
    Gh = [sg[j][:].bitcast(BF16)[:, 512:1024] for j in range(2)]
    ga = [xn[0][:, j * 512:(j + 1) * 512] for j in range(4)]
    gax = [sb(f"gax{j}", [P, 512], BF16) for j in range(2)]
    GA = [gax[0][:], gax[1][:]]
    Gacc = [xn[1][:, j * 512:(j + 1) * 512] for j in range(2)]
    GAT = [xn[1][:, 1024 + j * 512:1024 + (j + 1) * 512].rearrange("p (j c) -> p j c", j=4) for j in range(2)]
    ghx = [sb(f"ghx{j}", [P, 512], BF16) for j in range(2)]
    zbx = [sb(f"zbx{j}", [P, 512], F32) for j in range(2)]
    ebx = [sb(f"ebx{j}", [P, 512], BF16) for j in range(2)]
    zb4 = [ctmp[0][:], ctmp[1][:], zbx[0][:], zbx[1][:]]
    zk4 = ["ctmp0", "ctmp1", "zbx0", "zbx1"]
    Eb4 = [Eb[0], Eb[1], ebx[0][:], ebx[1][:]]
    ek4 = ["Eb0", "Eb1", "ebx0", "ebx1"]
    NZ = 4
    Ghr = [Gh[0], Gh[1], ghx[0][:], ghx[1][:]]
    NGH = 4
    for j in range(2):
        S.alias(f"Ghr{j}", "SG%d" % j, 1024, 2048)
    cand = [tmpf[:, j * 256:(j + 1) * 256] for j in range(3)]
    wkb = tmpf[:, 768:896]
    for i in range(NT):
        if i < 2:
            S.alias(f"S12_{i}", "Y", i * 8192, (i + 1) * 8192)
        else:
            S.alias(f"S12_{i}", "G", (i - 2) * 8192, (i - 1) * 8192)
    S.alias("qTp", "G", 16384, 24576)
    S.alias("xn0", "XN0", 0, 4096)
    S.alias("xn1", "XN1", 0, 4096)
    S.alias("junk", "XN1", 0, 4096)
    for j in range(2):
        S.alias(f"ga{j}", "XN0", j * 1024, (j + 1) * 1024)
        S.alias(f"ga{j + 2}", "XN0", (j + 2) * 1024, (j + 3) * 1024)
        S.alias(f"Gacc{j}", "XN1", j * 1024, (j + 1) * 1024)
        S.alias(f"GAT{j}", "XN1", 2048 + j * 1024, 2048 + (j + 1) * 1024)
        S.alias(f"sg{j}", "SG%d" % j, 0, 2048)
        S.alias(f"Eb{j}", "SG%d" % j, 0, 1024)
        S.alias(f"Gh{j}", "SG%d" % j, 1024, 2048)

    G32 = Greg[:, 0:8192].bitcast(F32).rearrange("p (i c) -> p i c", i=NT)
    vg = Greg[:, 8192:12288].rearrange("p (i c) -> p i c", i=NT)
    vaug = Greg[:, 0:4112].rearrange("p (i h c) -> p i h c", i=NT, h=4)
    so = Greg[:, 4224:8320].rearrange("p (i c) -> p i c", i=NT)
    pre = [Greg[:, 8448 + j * 1040: 8448 + j * 1040 + 1030].bitcast(F32) for j in range(2)]
    ymix = Yreg[:].rearrange("p (i c) -> p i c", i=NT)

    for i in range(NT):
        S.alias(f"G32_{i}", "G", i * 4096, (i + 1) * 4096)
        S.alias(f"vg{i}", "G", 16384 + i * 2048, 16384 + (i + 1) * 2048)
        S.alias(f"vaug{i}", "G", i * 2056, (i + 1) * 2056)
        S.alias(f"so{i}", "G", 8448 + i * 2048, 8448 + (i + 1) * 2048)
        S.alias(f"ymix{i}", "Y", i * 4096, (i + 1) * 4096)
    for j in range(2):
        S.alias(f"pre{j}", "G", 16896 + j * 2080, 16896 + j * 2080 + 2060)

    PTt = ps("PT", [P, 1024])
    PPt = ps("PP", [P, 1024])
    PBt = ps("PB", [P, 2048])
    PT = PTt[:].bitcast(BF16).rearrange("p (k c) -> p k c", k=KD)
    PP = [PPt[:, 0:512], PPt[:, 512:1024]]
    PM = PBt[:, 0:1024]
    PS = PBt[:, 1024:1536]
    PX = PBt[:, 1536:2048]
    PXbf = PBt[:, 1536 + 128:1536 + 384].bitcast(BF16).rearrange("p (h c) -> p h c", h=4)

    cnt = {"pp": 0, "w": 0, "xn": 0, "pre": 0, "ct": 0, "ga": 0}

    def nxt(name, n):
        v = cnt[name] % n
        cnt[name] += 1
        return v

    def ld(eng, out, in_, key, slot):
        S.op(eng, lambda e: e.dma_start(out=out, in_=in_), r=(), w=(key,), dma="c_" + key)

    ld("pool", ident[:], ident_d, "ident", "c0")
    ld("pool", maskT[:], maskT_d, "maskT", "c0")
    ld("pool", wsT[:], wsT_d, "wsT", "c0")
    ld("pool", kT12[:], kT12_d, "kT12", "c0")
    ld("sp", tri32[:], maskT_d, "tri32", "c1")
    ld("sp", g1T[:], g1T_d, "g1T", "c1")
    ld("sp", g2T[:], g2T_d, "g2T", "c1")
    ld("sp", gmixT[:], gmixT_d, "gmixT", "c1")
    ld("sp", gvn_rep[:], gvn_d.partition_broadcast(P), "gvn_rep", "c1")
    ld("sp", bsp[:], bsp_d, "bsp", "c1")
    ld("sp", cw[:], cw_d, "cw", "c1")
    ld("sp", cb[:], cb_d, "cb", "c1")
    ld("sp", bi_rep[:], bi_d.partition_broadcast(P), "bi_rep", "c1")
    ld("sp", bf_rep[:], bf_d.partition_broadcast(P), "bf_rep", "c1")
    ld("sp", flag[:], flag_d, "flag", "c1")
    S.op("dve", lambda e: e.memset(ones32[:], 1.0), w=("ones32",))
    S.op("dve", lambda e: e.memset(Cst[:], 0.0), w=("Cst",))
    S.op("dve", lambda e: e.memset(hist[:], 0.0), w=("hist",))
    S.op("dve", lambda e: e.tensor_tensor(out=wsT[:], in0=wsT[:], in1=maskT[:].unsqueeze(1).to_broadcast([P, 8, P]), op=ALU.mult),
         r=("wsT", "maskT"), w=("wsT",))

    def wload(src_ap, ncols=512):
        slot = nxt("w", 3)
        wt = Wring[slot]
        dst = wt[:, :, 0:ncols]
        S.op("pool", lambda e: e.dma_start(out=dst, in_=src_ap), w=(f"W{slot}",), dma=f"W{slot}")
        return wt, f"W{slot}"

    def rms_rstd(src_ap, key_src, col, n):
        S.op("act", lambda e: e.activation(out=junk[:, 0:n], in_=src_ap, func=AF.Square, accum_out=ssq[:, col:col + 1]),
             r=(key_src,), w=("junk", f"ssq{col}"))
        S.op("act", lambda e: e.activation(out=ssq[:, col:col + 1], in_=ssq[:, col:col + 1], func=AF.Sqrt, scale=1.0 / n, bias=eps_t[:, 0:1]),
             r=(f"ssq{col}", "eps"), w=(f"ssq{col}",))
        S.op("dve", lambda e: e.reciprocal(out=rstd[:, col:col + 1], in_=ssq[:, col:col + 1]), r=(f"ssq{col}",), w=(f"rstd{col}",))

    eps_t = sb("eps_t", [P, 1], F32)
    S.op("dve", lambda e: e.memset(eps_t[:], EPS), w=("eps",))

    def norm_transpose(i, src_ap, key_src, gT, gkey, dst_keyname):
        col = i
        rms_rstd(src_ap, key_src, col, D)
        xs = nxt("xn", 2)
        xnb = xn[xs]
        S.op("dve", lambda e: e.tensor_scalar(out=xnb[:], in0=src_ap, scalar1=rstd[:, col:col + 1], scalar2=None, op0=ALU.mult),
             r=(key_src, f"rstd{col}"), w=(f"xn{xs}",))
        for k in range(KD):
            S.op("pe", lambda e, k=k: e.transpose(out=PT[:, k, :], in_=xnb[:, k * P:(k + 1) * P], identity=ident[:]),
                 r=(f"xn{xs}", "ident"), w=("PT0", "PT1"))
        S.op("dve", lambda e: e.tensor_tensor(out=Hreg[:, :, i * P:(i + 1) * P], in0=PT, in1=gT[:].unsqueeze(2).to_broadcast([P, KD, P]), op=ALU.mult),
             r=("PT0", "PT1", gkey), w=(f"{dst_keyname}{i}",))

    def proj_tok(i, hkey, wt, wkey, ncols):
        s = nxt("pp", 2)
        out = PP[s][:, 0:ncols]
        for k in range(KD):
            S.op("pe", lambda e, k=k: e.matmul(out, lhsT=Hreg[:, k, i * P:(i + 1) * P], rhs=wt[:, k, 0:ncols], start=(k == 0), stop=(k == KD - 1)),
                 r=(f"{hkey}{i}", wkey), w=(f"PP{s}",))
        return PP[s], f"PP{s}"

    def proj_feat(hkey, wt, wkey, c0):
        s = nxt("pp", 2)
        out = PP[s]
        for k in range(KD):
            S.op("pe", lambda e, k=k: e.matmul(out, lhsT=wt[:, k, c0:c0 + P], rhs=Hreg[:, k, :], start=(k == 0), stop=(k == KD - 1)),
                 r=tuple(f"{hkey}{i}" for i in range(NT)) + (wkey,), w=(f"PP{s}",))
        return PP[s], f"PP{s}"

    def dbg_dump(name, ap, shape, key):
        if name not in dbg:
            return
        t = nc.dram_tensor("dbg_" + name, list(shape), F32, kind="ExternalOutput").ap()
        dbg_out[name] = t
        S.op("sp", lambda e: e.dma_start(out=t, in_=ap), r=key, dma="dbg")

    def phaseA(blk, is_pre, last_pre):
        xsrc = xp if is_pre else xo
        b0 = (blk if is_pre else blk - nb_pre) * TB
        for i in range(NT):
            S.op("sp", lambda e, i=i: e.dma_start(out=xres[:, i, :], in_=xsrc[b0 + i * P: b0 + (i + 1) * P, :]),
                 w=(f"xres{i}",), dma=f"x{i}")
            norm_transpose(i, xres[:, i, :], f"xres{i}", g1T, "g1T", "hT")
        if "hT" in dbg and not is_pre and blk == nb_pre:
            pass

        wt, wk_ = wload(w_in_v[:, :, 5120:5128], 8)
        for i in range(NT):
            pp, pk = proj_tok(i, "hT", wt, wk_, 8)
            S.op("act", lambda e, i=i, pp=pp: e.copy(out=if_sb[:, i, :], in_=pp[:, 0:8]), r=(pk,), w=(f"if{i}",))
            g = gsm[:, i, :]
            S.op("dve", lambda e, i=i, g=g: e.tensor_tensor(out=g[:, 20:24], in0=if_sb[:, i, 4:8], in1=bf_rep[:], op=ALU.add),
                 r=(f"if{i}", "bf_rep"), w=(f"g{i}",))
            S.op("act", lambda e, g=g: e.activation(out=g[:, 20:24], in_=g[:, 20:24], func=AF.Exp, scale=-1.0), r=(f"g{i}",), w=(f"g{i}",))
            S.op("act", lambda e, g=g: e.activation(out=g[:, 0:4], in_=g[:, 20:24], func=AF.Ln, bias=one_t[:, 0:1]), r=(f"g{i}", "one"), w=(f"g{i}",))
            S.op("pe", lambda e, g=g: e.matmul(PX[:, 0:4], lhsT=tri32[:], rhs=g[:, 0:4], start=True, stop=True), r=(f"g{i}", "tri32"), w=("PX",))
            S.op("pe", lambda e, g=g: e.matmul(PX[:, 4:8], lhsT=ones32[:], rhs=g[:, 0:4], start=True, stop=True), r=(f"g{i}", "ones32"), w=("PX",))
            S.op("act", lambda e, g=g: e.copy(out=g[:, 24:32], in_=PX[:, 0:8]), r=("PX",), w=(f"g{i}",))
            S.op("dve", lambda e, g=g: e.tensor_tensor(out=g[:, 4:8], in0=g[:, 24:28], in1=g[:, 28:32], op=ALU.subtract), r=(f"g{i}",), w=(f"g{i}",))
            S.op("dve", lambda e, i=i, g=g: e.tensor_tensor(out=g[:, 20:24], in0=if_sb[:, i, 0:4], in1=bi_rep[:], op=ALU.add), r=(f"if{i}", "bi_rep", f"g{i}"), w=(f"g{i}",))
            S.op("dve", lambda e, g=g: e.tensor_tensor(out=g[:, 20:24], in0=g[:, 20:24], in1=g[:, 4:8], op=ALU.add), r=(f"g{i}",), w=(f"g{i}",))
            S.op("act", lambda e, g=g: e.activation(out=g[:, 8:12], in_=g[:, 20:24], func=AF.Exp), r=(f"g{i}",), w=(f"g{i}",))
            S.op("act", lambda e, g=g: e.activation(out=g[:, 12:16], in_=g[:, 4:8], func=AF.Exp), r=(f"g{i}",), w=(f"g{i}",))
            S.op("act", lambda e, g=g: e.activation(out=g[:, 16:20], in_=g[:, 28:32], func=AF.Exp, scale=-1.0), r=(f"g{i}",), w=(f"g{i}",))

        if not is_pre:
            for half in range(2):
                wt, wk_ = wload(w_in_v[:, :, 1024 + half * 512: 1024 + (half + 1) * 512])
                for i in range(NT):
                    pp, pk = proj_tok(i, "hT", wt, wk_, 512)
                    S.op("act", lambda e, i=i, pp=pp, half=half: e.activation(out=G32[:, i, half * 512:(half + 1) * 512], in_=pp, func=AF.Gelu_apprx_tanh),
                         r=(pk,), w=(f"G32_{i}",))
            for i in range(NT):
                rms_rstd(G32[:, i, :], f"G32_{i}", 4 + i, 1024)
                S.op("dve", lambda e, i=i: e.scalar_tensor_tensor(out=vg[:, i, :], in0=G32[:, i, :], scalar=rstd[:, 4 + i:5 + i], in1=gvn_rep[:], op0=ALU.mult, op1=ALU.mult),
                     r=(f"G32_{i}", f"rstd{4 + i}", "gvn_rep"), w=(f"vg{i}",))
            for i in range(NT):
                for h in range(8):
                    S.op("pe", lambda e, i=i, h=h: e.matmul(PM[:, h * P:(h + 1) * P], lhsT=wsT[:, h, :], rhs=vg[:, i, h * P:(h + 1) * P], start=True, stop=True),
                         r=("wsT", f"vg{i}"), w=("PMa", "PMb",))
                S.op("dve", lambda e, i=i: e.tensor_tensor(out=G32[:, i, :].rearrange("p (h c) -> p h c", h=8), in0=PM.rearrange("p (h c) -> p h c", h=8),
                                                           in1=bsp[:].unsqueeze(2).to_broadcast([P, 8, P]), op=ALU.add),
                     r=("PMa", "PMb", "bsp", f"vg{i}"), w=(f"G32_{i}",))
            for half in range(2):
                wt, wk_ = wload(w_in_v[:, :, half * 512:(half + 1) * 512])
                for i in range(NT):
                    pp, pk = proj_tok(i, "hT", wt, wk_, 512)
                    c = nxt("ct", 2)
                    S.op("act", lambda e, pp=pp, c=c: e.activation(out=ctmp[c][:], in_=pp, func=AF.Gelu_apprx_tanh), r=(pk,), w=(f"ctmp{c}",))
                    S.op("dve", lambda e, i=i, half=half, c=c: e.tensor_tensor(out=G32[:, i, half * 512:(half + 1) * 512], in0=G32[:, i, half * 512:(half + 1) * 512], in1=ctmp[c][:], op=ALU.mult),
                         r=(f"ctmp{c}", f"G32_{i}"), w=(f"G32_{i}",))
            for i in range(NT):
                for h in range(8):
                    S.op("act", lambda e, i=i, h=h: e.activation(out=junk[:, 0:P], in_=G32[:, i, h * P:(h + 1) * P], func=AF.Square, accum_out=hsm[:, h:h + 1]),
                         r=(f"G32_{i}",), w=("junk", "hsm"))
                S.op("act", lambda e: e.activation(out=hsm[:, 0:8], in_=hsm[:, 0:8], func=AF.Sqrt, scale=1.0 / P, bias=eps_t[:, 0:1]), r=("hsm", "eps"), w=("hsm",))
                S.op("dve", lambda e: e.reciprocal(out=hsm[:, 8:16], in_=hsm[:, 0:8]), r=("hsm",), w=("hsm",))
                S.op("dve", lambda e, i=i: e.tensor_tensor(out=ymix[:, i, 0:1024].rearrange("p (h c) -> p h c", h=8), in0=G32[:, i, :].rearrange("p (h c) -> p h c", h=8),
                                                           in1=hsm[:, 8:16].unsqueeze(2).to_broadcast([P, 8, P]), op=ALU.mult),
                     r=(f"G32_{i}", "hsm"), w=(f"ymix{i}",))

        chunks = list(range(8)) if (not is_pre or last_pre) else list(range(4, 8))
        for part in ((0, 1) if (not is_pre or last_pre) else (1,)):
            wt, wk_ = wload(w_in_v[:, :, 2048 + part * 512: 2048 + (part + 1) * 512])
            for jj in range(4):
                j = part * 4 + jj
                pp, pk = proj_feat("hT", wt, wk_, jj * P)
                pr = nxt("pre", 2)
                pb = pre[pr]
                S.op("act", lambda e, pb=pb, pp=pp: e.copy(out=pb[:, 3:515], in_=pp), r=(pk,), w=(f"pre{pr}",))
                S.op("dve", lambda e, pb=pb, j=j: e.tensor_copy(out=pb[:, 0:3], in_=hist[:, j, :]), r=("hist",), w=(f"pre{pr}",))
                S.op("dve", lambda e, pb=pb, j=j: e.tensor_copy(out=hist[:, j, :], in_=pb[:, 512:515]), r=(f"pre{pr}",), w=("hist",))
                if is_pre and part == 0:
                    continue
                c = nxt("ct", 2)
                ct = ctmp[c]
                S.op("dve", lambda e, pb=pb, j=j, ct=ct: e.tensor_scalar(out=ct[:], in0=pb[:, 0:512], scalar1=cw[:, j, 0:1], scalar2=cb[:, j:j + 1], op0=ALU.mult, op1=ALU.add),
                     r=(f"pre{pr}", "cw", "cb"), w=(f"ctmp{c}",))
                for tap in range(1, 4):
                    S.op("dve", lambda e, pb=pb, j=j, ct=ct, tap=tap: e.scalar_tensor_tensor(out=ct[:], in0=pb[:, tap:tap + 512], scalar=cw[:, j, tap:tap + 1], in1=ct[:], op0=ALU.mult, op1=ALU.add),
                         r=(f"pre{pr}", "cw", f"ctmp{c}"), w=(f"ctmp{c}",))
                S.op("act", lambda e, ct=ct, c=c: e.activation(out=sg[c][:], in_=ct[:], func=AF.Sigmoid), r=(f"ctmp{c}",), w=(f"sg{c}",))
                if part == 0:
                    S.op("dve", lambda e, ct=ct, c=c, jj=jj: e.tensor_tensor(out=qT[:, jj, :], in0=ct[:], in1=sg[c][:], op=ALU.mult),
                         r=(f"ctmp{c}", f"sg{c}"), w=("qT",))
                else:
                    S.op("dve", lambda e, ct=ct, c=c, jj=jj: e.scalar_tensor_tensor(out=kT[:, jj, :], in0=ct[:], scalar=float(P ** -0.5), in1=sg[c][:], op0=ALU.mult, op1=ALU.mult),
                         r=(f"ctmp{c}", f"sg{c}"), w=("kT",))

        for half in range(2):
            wt, wk_ = wload(w_in_v[:, :, 3072 + half * 512: 3072 + (half + 1) * 512])
            for i in range(NT):
                pp, pk = proj_tok(i, "hT", wt, wk_, 512)
                S.op("dve", lambda e, i=i, half=half, pp=pp: e.tensor_tensor(out=vaug[:, i, 2 * half:2 * half + 2, 0:256], in0=pp.rearrange("p (h c) -> p h c", h=2),
                                                                          in1=gsm[:, i, 8 + 2 * half:10 + 2 * half].unsqueeze(2).to_broadcast([P, 2, 256]), op=ALU.mult),
                     r=(pk, f"g{i}"), w=(f"vaug{i}",))
        for i in range(NT):
            S.op("dve", lambda e, i=i: e.tensor_copy(out=vaug[:, i, :, 256:257], in_=gsm[:, i, 8:12].unsqueeze(2)), r=(f"g{i}",), w=(f"vaug{i}",))
        if not is_pre:
            for half in range(2):
                wt, wk_ = wload(w_in_v[:, :, 4096 + half * 512: 4096 + (half + 1) * 512])
                for i in range(NT):
                    pp, pk = proj_tok(i, "hT", wt, wk_, 512)
                    S.op("act", lambda e, i=i, half=half, pp=pp: e.activation(out=so[:, i, half * 512:(half + 1) * 512], in_=pp, func=AF.Sigmoid), r=(pk,), w=(f"so{i}",))

        for i in range(NT):
            tsl = slice(i * P, (i + 1) * P)
            g = gsm[:, i, :]
            for h in range(4):
                S.op("pe", lambda e, h=h, tsl=tsl: e.transpose(out=PXbf[:, h, :], in_=kT[:, h, tsl], identity=ident[:]), r=("kT", "ident"), w=("PX",))
            S.op("act", lambda e, i=i: e.copy(out=ktok[:, i, :, :], in_=PXbf), r=("PX",), w=(f"ktok{i}",))
            S.op("dve", lambda e, g=g: e.tensor_tensor(out=Cs_f[:], in0=Cst[:], in1=g[:, 16:20].unsqueeze(2).to_broadcast([P, 4, 257]), op=ALU.mult),
                 r=("Cst", f"g{i}"), w=("Cs_f",))
            if not is_pre:
                S.op("act", lambda e: e.copy(out=Cs_bf[:], in_=Cs_f[:]), r=("Cs_f",), w=("Cs_bf",))
                for h in range(4):
                    S.op("pe", lambda e, h=h, tsl=tsl: e.matmul(PS[:, h * P:(h + 1) * P], lhsT=kT[:, h, tsl], rhs=qT[:, h, tsl], start=True, stop=True),
                         r=("kT", "qT"), w=("PS",))
                S.op("dve", lambda e: e.tensor_tensor(out=sTm[:], in0=PS.rearrange("p (h c) -> p h c", h=4), in1=maskT[:].unsqueeze(1).to_broadcast([P, 4, P]), op=ALU.mult),
                     r=("PS", "maskT"), w=("sTm",))
                for h in range(4):
                    S.op("pe", lambda e, h=h, i=i: e.matmul(PM[:, h * 256:(h + 1) * 256], lhsT=sTm[:, h, :], rhs=vaug[:, i, h, 0:256], start=True, stop=False),
                         r=("sTm", f"vaug{i}"), w=("PMa", "PMb",))
                    S.op("pe", lambda e, h=h, tsl=tsl: e.matmul(PM[:, h * 256:(h + 1) * 256], lhsT=qT[:, h, tsl], rhs=Cs_bf[:, h, 0:256], start=False, stop=True),
                         r=("qT", "Cs_bf"), w=("PMa", "PMb",))
                    S.op("pe", lambda e, h=h, i=i: e.matmul(PX[:, 8 + h:9 + h], lhsT=sTm[:, h, :], rhs=vaug[:, i, h, 256:257], start=True, stop=False),
                         r=("sTm", f"vaug{i}"), w=("PX",))
                    S.op("pe", lambda e, h=h, tsl=tsl: e.matmul(PX[:, 8 + h:9 + h], lhsT=qT[:, h, tsl], rhs=Cs_bf[:, h, 256:257], start=False, stop=True),
                         r=("qT", "Cs_bf"), w=("PX",))
            PPC = PPt[:].rearrange("p (h c) -> p h c", h=4)
            for h in range(4):
                S.op("pe", lambda e, h=h, i=i: e.matmul(PPt[:, h * 256:(h + 1) * 256], lhsT=ktok[:, i, h, :], rhs=vaug[:, i, h, 0:256], start=True, stop=True),
                     r=(f"ktok{i}", f"vaug{i}"), w=("PP0", "PP1"))
                S.op("pe", lambda e, h=h, i=i: e.matmul(PX[:, 12 + h:13 + h], lhsT=ktok[:, i, h, :], rhs=vaug[:, i, h, 256:257], start=True, stop=True),
                     r=(f"ktok{i}", f"vaug{i}"), w=("PX",))
            S.op("dve", lambda e: e.tensor_tensor(out=Cst[:, :, 0:256], in0=Cs_f[:, :, 0:256], in1=PPC, op=ALU.add), r=("Cs_f", "PP0", "PP1"), w=("Cst",))
            S.op("dve", lambda e: e.tensor_tensor(out=Cst[:, :, 256:257], in0=Cs_f[:, :, 256:257], in1=PX[:, 12:16].unsqueeze(2), op=ALU.add), r=("Cs_f", "PX"), w=("Cst",))
            if is_pre:
                continue
            S.op("act", lambda e: e.activation(out=hsm[:, 16:20], in_=PX[:, 8:12], func=AF.Abs), r=("PX",), w=("hsm2",))
            S.op("dve", lambda e, g=g: e.tensor_tensor(out=hsm[:, 16:20], in0=hsm[:, 16:20], in1=g[:, 12:16], op=ALU.max), r=("hsm2", f"g{i}"), w=("hsm2",))
            S.op("dve", lambda e: e.reciprocal(out=hsm[:, 20:24], in_=hsm[:, 16:20]), r=("hsm2",), w=("hsm2",))
            S.op("dve", lambda e: e.tensor_tensor(out=tmpf[:].rearrange("p (h c) -> p h c", h=4), in0=PM.rearrange("p (h c) -> p h c", h=4),
                                                  in1=hsm[:, 20:24].unsqueeze(2).to_broadcast([P, 4, 256]), op=ALU.mult), r=("PMa", "PMb", "hsm2"), w=("tmpf",))
            S.op("dve", lambda e, i=i: e.tensor_tensor(out=tmpf[:], in0=tmpf[:], in1=so[:, i, :], op=ALU.mult), r=("tmpf", f"so{i}"), w=("tmpf",))
            for h in range(4):
                S.op("act", lambda e, h=h: e.activation(out=junk[:, 0:256], in_=tmpf[:, h * 256:(h + 1) * 256], func=AF.Square, accum_out=hsm[:, 24 + h:25 + h]),
                     r=("tmpf",), w=("junk", "hsm3"))
            S.op("act", lambda e: e.activation(out=hsm[:, 24:28], in_=hsm[:, 24:28], func=AF.Sqrt, scale=1.0 / 256, bias=eps_t[:, 0:1]), r=("hsm3", "eps"), w=("hsm3",))
            S.op("dve", lambda e: e.reciprocal(out=hsm[:, 28:32], in_=hsm[:, 24:28]), r=("hsm3",), w=("hsm3",))
            S.op("dve", lambda e, i=i: e.tensor_tensor(out=ymix[:, i, 1024:2048].rearrange("p (h c) -> p h c", h=4), in0=tmpf[:].rearrange("p (h c) -> p h c", h=4),
                                                       in1=hsm[:, 28:32].unsqueeze(2).to_broadcast([P, 4, 256]), op=ALU.mult), r=("tmpf", "hsm3"), w=(f"ymix{i}",))
        if is_pre:
            return
        for i in range(NT):
            for k in range(KD):
                S.op("pe", lambda e, k=k, i=i: e.transpose(out=PT[:, k, :], in_=ymix[:, i, k * P:(k + 1) * P], identity=ident[:]),
                     r=(f"ymix{i}", "ident"), w=("PT0", "PT1"))
            S.op("dve", lambda e, i=i: e.tensor_tensor(out=Hreg[:, :, i * P:(i + 1) * P], in0=PT, in1=gmixT[:].unsqueeze(2).to_broadcast([P, KD, P]), op=ALU.mult),
                 r=("PT0", "PT1", "gmixT"), w=(f"hT{i}",))
        for gcol in range(4):
            wt, wk_ = wload(w_out_v[:, :, gcol * 512:(gcol + 1) * 512])
            for i in range(NT):
                pp, pk = proj_tok(i, "hT", wt, wk_, 512)
                S.op("dve", lambda e, i=i, gcol=gcol, pp=pp: e.tensor_tensor(out=xres[:, i, gcol * 512:(gcol + 1) * 512], in0=xres[:, i, gcol * 512:(gcol + 1) * 512], in1=pp, op=ALU.add),
                     r=(pk, f"xres{i}"), w=(f"xres{i}",))

    one_t = sb("one_t", [P, 1], F32)
    S.op("dve", lambda e: e.memset(one_t[:], 1.0), w=("one",))

    def phaseP(blk):
        for i in range(NT):
            norm_transpose(i, xres[:, i, :], f"xres{i}", g2T, "g2T", "hT")
        for part in range(2):
            wt, wk_ = wload(wq_v[:, :, part * 512:(part + 1) * 512])
            for hh in range(4):
                h = part * 4 + hh
                pp, pk = proj_feat("hT", wt, wk_, hh * P)
                S.op("act", lambda e, h=h, pp=pp: e.copy(out=qTp[:, h, :], in_=pp), r=(pk,), w=("qTp",))
        for i in range(NT):
            tsl = slice(i * P, (i + 1) * P)
            for hp in range(2):
                for hh in range(4):
                    h = hp * 4 + hh
                    S.op("pe", lambda e, h=h, hh=hh, tsl=tsl: e.matmul(PM[:, hh * 256:(hh + 1) * 256], lhsT=qTp[:, h, tsl], rhs=kT12[:, h, :], start=True, stop=True),
                         r=("qTp", "kT12"), w=("PMa", "PMb",))
                S.op("act", lambda e, i=i, hp=hp: e.copy(out=S12[i][:, hp * 4:(hp + 1) * 4, :], in_=PM.rearrange("p (h c) -> p h c", h=4)),
                     r=("PMa", "PMb",), w=(f"S12_{i}",))
        for i in range(NT):
            for h in range(8):
                for half in range(2):
                    src = S12[i][:, h, half * P:(half + 1) * P]
                    o = half * 16
                    S.op("dve", lambda e, src=src, o=o: e.max(out=tv[:, o:o + 8], in_=src), r=(f"S12_{i}",), w=("tv",))
                    S.op("dve", lambda e, src=src, o=o: e.match_replace(out=wkb, in_to_replace=tv[:, o:o + 8], in_values=src, imm_value=NEG),
                         r=(f"S12_{i}", "tv"), w=("tmpf",))
                    S.op("dve", lambda e, o=o: e.max(out=tv[:, o + 8:o + 16], in_=wkb), r=("tmpf",), w=("tv",))
                S.op("dve", lambda e: e.tensor_tensor(out=cand[0].rearrange("p (a b) -> p a b", a=16), in0=tv[:, 0:16].unsqueeze(2).to_broadcast([P, 16, 16]),
                                                      in1=tv[:, 16:32].unsqueeze(1).to_broadcast([P, 16, 16]), op=ALU.add), r=("tv",), w=("tmpf",))
                S.op("dve", lambda e, h=h: e.max(out=svall[:, h, 0:8], in_=cand[0]), r=("tmpf",), w=("svall",))
                S.op("dve", lambda e, h=h: e.match_replace(out=cand[1], in_to_replace=svall[:, h, 0:8], in_values=cand[0], imm_value=NEG), r=("tmpf", "svall"), w=("tmpf",))
                S.op("dve", lambda e, h=h: e.max(out=svall[:, h, 8:16], in_=cand[1]), r=("tmpf",), w=("svall",))
                S.op("dve", lambda e, h=h: e.match_replace(out=cand[2], in_to_replace=svall[:, h, 8:16], in_values=cand[1], imm_value=NEG), r=("tmpf", "svall"), w=("tmpf",))
                S.op("dve", lambda e, h=h: e.max(out=svall[:, h, 16:24], in_=cand[2]), r=("tmpf",), w=("svall",))
            S.op("dve", lambda e: e.tensor_tensor(out=pst[:, :, 0:16], in0=svall[:, :, 0:16], in1=svall[:, :, 0:1].to_broadcast([P, 8, 16]), op=ALU.subtract),
                 r=("svall",), w=("pst",))
            S.op("act", lambda e: e.activation(out=pst[:, :, 0:16], in_=pst[:, :, 0:16], func=AF.Exp), r=("pst",), w=("pst",))
            S.op("dve", lambda e: e.reduce_sum(out=pst[:, :, 16:17], in_=pst[:, :, 0:16], axis=AX.X), r=("pst",), w=("pst",))
            S.op("act", lambda e: e.activation(out=pst[:, :, 17:18], in_=pst[:, :, 16:17], func=AF.Ln), r=("pst",), w=("pst",))
            S.op("dve", lambda e, i=i: e.scalar_tensor_tensor(out=psc[:, i, :, 0:1], in0=svall[:, :, 0:1], scalar=-1.0, in1=pst[:, :, 17:18], op0=ALU.mult, op1=ALU.subtract),
                 r=("svall", "pst"), w=(f"psc{i}",))
            S.op("dve", lambda e: e.tensor_tensor(out=pst[:, :, 18:19], in0=svall[:, :, 15:16], in1=svall[:, :, 16:17], op=ALU.add), r=("svall", "pst"), w=("pst",))
            S.op("dve", lambda e, i=i: e.scalar_tensor_tensor(out=psc[:, i, :, 1:2], in0=pst[:, :, 18:19], scalar=0.5, in1=psc[:, i, :, 0:1], op0=ALU.mult, op1=ALU.add),
                 r=("pst", f"psc{i}"), w=(f"psc{i}",))
            S.op("dve", lambda e, i=i: e.tensor_scalar(out=psc[:, i, :, 2:3], in0=psc[:, i, :, 1:2], scalar1=-1.0, scalar2=None, op0=ALU.mult),
                 r=(f"psc{i}",), w=(f"psc{i}",))
        units = [(g, i) for g in range(NG) for i in range(NT)]
        NU = len(units)
        wts = {}

        def load_u(g):
            wts[("u", g)] = wload(UT_d[:, :, g * GE:(g + 1) * GE])

        def load_v(g):
            slot = nxt("w", 3)
            vt = Wring[slot][:].rearrange("p k c -> p (k c)").rearrange("p (j d) -> p j d", j=GC)
            vk = f"W{slot}"
            S.op("pool", lambda e, vt=vt, g=g: e.dma_start(out=vt, in_=V_v[:, g * GC:(g + 1) * GC, :]), w=(vk,), dma=vk)
            wts[("v", g)] = (vt, vk)

        st = {}

        def stageA_pe(u):
            g, i = units[u]
            ut, uk = wts[("u", g)]
            sl = u % 2
            for k in range(KD):
                S.op("pe", lambda e, k=k, i=i, ut=ut, sl=sl: e.matmul(PP[sl], lhsT=Hreg[:, k, i * P:(i + 1) * P], rhs=ut[:, k, 0:GE], start=(k == 0), stop=(k == KD - 1)),
                     r=(f"hT{i}", uk), w=(f"PP{sl}",))

        def gelu_pair(kp):
            s0 = (2 * kp) % 4
            S.op("act", lambda e, s0=s0: e.activation(out=xn[0][:, s0 * 512:(s0 + 2) * 512], in_=PPt[:, 0:1024], func=AF.Gelu_apprx_tanh),
                 r=("PP0", "PP1"), w=(f"ga{s0}", f"ga{s0 + 1}"))

        DVEH = DVEH_CFG
        BIGA = 1.0e7

        def Z(n):
            u, h = divmod(n, 8)
            g, i = units[u]
            a = u % 2
            c = n % NZ
            S.op("dve", lambda e, i=i, h=h, c=c, g=g: e.scalar_tensor_tensor(
                out=zb4[c].rearrange("p (j c) -> p j c", j=GC),
                in0=S12[i][:, h, P:2 * P].unsqueeze(1).to_broadcast([P, GC, P]),
                scalar=psc[:, i, h, 0:1],
                in1=S12[i][:, h, g * GC:(g + 1) * GC].unsqueeze(2).to_broadcast([P, GC, P]),
                op0=ALU.add, op1=ALU.add), r=(f"S12_{i}", f"psc{i}"), w=(zk4[c],))
            if h not in DVEH:
                S.op("act", lambda e, i=i, h=h, c=c: e.activation(out=zb4[c], in_=zb4[c], func=AF.Prelu, bias=psc[:, i, h, 2:3], alpha=BIGA),
                     r=(zk4[c], f"psc{i}"), w=(zk4[c],))
                S.op("act", lambda e, i=i, h=h, c=c: e.activation(out=Eb4[c], in_=zb4[c], func=AF.Exp, bias=psc[:, i, h, 1:2]),
                     r=(zk4[c], f"psc{i}"), w=(ek4[c],))
            else:
                S.op("act", lambda e, c=c: e.activation(out=Eb4[c], in_=zb4[c], func=AF.Exp), r=(zk4[c],), w=(ek4[c],))

        PSG = [PS, PX]
        psgk = ["PS", "PX"]

        def M(n):
            u, h = divmod(n, 8)
            g, i = units[u]
            a = u % 2
            c = n % NZ
            if h not in DVEH:
                src, sk = Eb4[c], ek4[c]
            else:
                q = h % NGH
                S.op("dve", lambda e, i=i, h=h, c=c, q=q: e.scalar_tensor_tensor(out=Ghr[q], in0=zb4[c], scalar=psc[:, i, h, 1:2], in1=Eb4[c], op0=ALU.is_ge, op1=ALU.mult),
                     r=(zk4[c], ek4[c], f"psc{i}"), w=(f"Ghr{q}",))
                src, sk = Ghr[q], f"Ghr{q}"
            S.op("pe", lambda e, a=a, h=h, src=src: e.matmul(PSG[a], lhsT=ident[:], rhs=src, start=(h == 0), stop=(h == 7)),
                 r=(sk, "ident"), w=(psgk[a],))

        def GAop(u):
            a = u % 2
            s4 = u % 4
            S.op("dve", lambda e, a=a, s4=s4: e.tensor_tensor(out=GA[a], in0=PSG[a], in1=ga[s4], op=ALU.mult), r=(psgk[a], f"ga{s4}"), w=(f"GA{a}",))

        def stageC1(u):
            a = u % 2
            ptv = PT[:, 8 * a:8 * a + 4, :]
            for j in range(GC):
                S.op("pe", lambda e, a=a, j=j, ptv=ptv: e.transpose(out=ptv[:, j, :], in_=GA[a][:, j * P:(j + 1) * P], identity=ident[:]),
                     r=(f"GA{a}", "ident"), w=(f"PT{a}",))

        def stageC2(u):
            g, i = units[u]
            a = u % 2
            vt, vk = wts[("v", g)]
            ptv = PT[:, 8 * a:8 * a + 4, :]
            S.op("act", lambda e, a=a, ptv=ptv: e.copy(out=GAT[a], in_=ptv), r=(f"PT{a}",), w=(f"GAT{a}",))

            def vmm(half):
                for b in range(2):
                    dc = 2 * half + b
                    for j in range(GC):
                        S.op("pe", lambda e, a=a, j=j, dc=dc, b=b, vt=vt: e.matmul(PBt[:, b * 512:(b + 1) * 512], lhsT=GAT[a][:, j, :], rhs=vt[:, j, dc * 512:(dc + 1) * 512],
                                                                              start=(j == 0), stop=(j == GC - 1)),
                             r=(f"GAT{a}", vk), w=(("PMa",) if b == 0 else ("PMb",)))

            def add(half):
                S.op("dve", lambda e, i=i, half=half: e.tensor_tensor(out=xres[:, i, half * 1024:(half + 1) * 1024], in0=xres[:, i, half * 1024:(half + 1) * 1024], in1=PBt[:, 0:1024], op=ALU.add),
                     r=("PMa", "PMb", f"xres{i}"), w=(f"xres{i}",))
            vmm(0)
            return vmm, add

        def a_pe_with_loads(u):
            stageA_pe(u)
            g1, i1 = units[u]
            if i1 == NT - 1 and g1 + 1 < NG:
                load_v(g1 + 1)

        load_u(0)
        load_v(0)
        load_u(1)
        a_pe_with_loads(0)
        a_pe_with_loads(1)
        gelu_pair(0)
        a_pe_with_loads(2)
        a_pe_with_loads(3)
        LOOK = NZ - 1
        NH = 8 * NU
        for n in range(min(LOOK, NH)):
            Z(n)
        pending = None
        for n in range(NH):
            u, h = divmod(n, 8)
            if n + LOOK < NH:
                Z(n + LOOK)
            M(n)
            if h == 1 and pending is not None:
                pending[1](0)
                pending[0](1)
            if h == 3 and u >= 1:
                if u % 2 == 0:
                    gelu_pair(u // 2)
                pending[1](1)
                pending = None
                gp, ip = units[u - 1]
                if ip == NT - 1 and gp + 2 < NG:
                    load_u(gp + 2)
                if u % 2 == 0:
                    for uu in (u + 2, u + 3):
                        if uu < NU:
                            a_pe_with_loads(uu)
            if h == 7:
                GAop(u)
                stageC1(u)
                pending = stageC2(u)
        pending[1](0)
        pending[0](1)
        pending[1](1)
        gslot = nxt("w", 3)
        gfv = Wring[gslot][:].rearrange("p k c -> p (k c)")[:, 0:2 * D].bitcast(F32)
        gfk = f"W{gslot}"
        S.op("pool", lambda e: e.dma_start(out=gfv, in_=gf_d.partition_broadcast(P)), w=(gfk,), dma=gfk)
        for i in range(NT):
            rms_rstd(xres[:, i, :], f"xres{i}", 8 + i, D)
            S.op("dve", lambda e, i=i: e.scalar_tensor_tensor(out=xres[:, i, :], in0=xres[:, i, :], scalar=rstd[:, 8 + i:9 + i], in1=gfv, op0=ALU.mult, op1=ALU.mult),
                 r=(f"xres{i}", f"rstd{8 + i}", gfk), w=(f"xres{i}",))

    def phaseOut(blk):
        b0 = (blk - nb_pre) * TB
        for i in range(NT):
            S.op("sp", lambda e, i=i: e.dma_start(out=y_d[b0 + i * P: b0 + (i + 1) * P, :], in_=xres[:, i, :]), r=(f"xres{i}",), dma=f"out{i}")

    for blk in range(nb_pre + nb_own):
        is_pre = blk < nb_pre
        phaseA(blk, is_pre, is_pre and blk == nb_pre - 1)
        if is_pre and blk == nb_pre - 1:
            S.op("dve", lambda e: e.tensor_scalar(out=Cst[:], in0=Cst[:], scalar1=flag[:, 0:1], scalar2=None, op0=ALU.mult), r=("Cst", "flag"), w=("Cst",))
            S.op("dve", lambda e: e.tensor_scalar(out=hist[:], in0=hist[:], scalar1=flag[:, 0:1], scalar2=None, op0=ALU.mult), r=("hist", "flag"), w=("hist",))
        if not is_pre:
            if stop_after != "A":
                phaseP(blk)
            phaseOut(blk)

    finals = [f"out{i}" for i in range(NT)] + (["dbg"] if dbg_out else [])
    S.emit(finals)
    es.close()
    return nc, dbg_out


def prep_weights(inp):
    f = np.float32
    c = lambda a: np.ascontiguousarray(a, dtype=f)
    w = {}
    w["g1T"] = c(inp["norm1_g"].reshape(KD, P).T)
    w["g2T"] = c(inp["norm2_g"].reshape(KD, P).T)
    w["gmixT"] = c(np.concatenate([inp["gm_out_g"], inp["ml_out_g"]]).reshape(KD, P).T)
    w["gf"] = c(inp["final_g"])
    w["gvn"] = c(inp["gm_vnorm_g"])
    w["w_in"] = c(inp["w_in"])
    w["wsT"] = c(np.transpose(inp["w_spatial"], (2, 0, 1)))
    w["bsp"] = c(inp["b_spatial"].T)
    w["cw"] = c(np.transpose(inp["ml_conv_w"].reshape(4, 8, P), (2, 1, 0)))
    w["cb"] = c(inp["ml_conv_b"].reshape(8, P).T)
    w["bi"] = c(inp["ml_b_i"])
    w["bf"] = c(inp["ml_b_f"])
    w["w_out"] = c(inp["w_out"])
    w["wq"] = c(inp["peer_wq"])
    kt = np.zeros((P, 8, 256), f)
    kt[0:64, :, 0:128] = np.transpose(inp["peer_k1"], (2, 0, 1))
    kt[64:128, :, 128:256] = np.transpose(inp["peer_k2"], (2, 0, 1))
    w["kT12"] = kt
    w["UT"] = c(np.transpose(inp["peer_u"].T.reshape(KD, P, NEXP), (1, 0, 2)))
    w["V"] = c(inp["peer_v"])
    w["ident"] = np.eye(P, dtype=f)
    w["maskT"] = np.triu(np.ones((P, P), f))
    return w


def kernel(**inputs):
    x = np.asarray(inputs["x"], dtype=np.float32)
    w = prep_weights(inputs)
    half = SEQ // 2
    nb = half // TB
    nc, _ = build(nb, nb)
    in_maps = []
    for c in range(NCORES):
        b, hf = c // 2, c % 2
        own = np.ascontiguousarray(x[b, hf * half:(hf + 1) * half])
        prev = np.ascontiguousarray(x[b, 0:half]) if hf == 1 else own
        m = dict(w)
        m["xo"] = own
        m["xp"] = prev
        m["flag"] = np.full((P, 1), float(hf), np.float32)
        in_maps.append(m)
    res = run_bass_kernel_spmd(nc, in_maps, core_ids=list(range(NCORES)))
    out = np.empty((BATCH, SEQ, D), np.float32)
    for c in range(NCORES):
        b, hf = c // 2, c % 2
        out[b, hf * half:(hf + 1) * half] = res.results[c]["y"]
    return out
```

```python
import numpy as np
from contextlib import ExitStack
import concourse.bass as bass
import concourse.mybir as mybir
from concourse.bass_utils import run_bass_kernel_spmd

F32 = mybir.dt.float32
BF16 = mybir.dt.bfloat16
AF = mybir.ActivationFunctionType
ALU = mybir.AluOpType
AX = mybir.AxisListType

P = 128
D = 2048
KD = 16
TB = 512
NT = 4
PROJ_W = 5128
EPS = 1e-6
NCORES = 8
SEQ = 8192
BATCH = 4
NEXP = 16384
GC = 4
NG = 128 // GC
GE = GC * 128
NEG = -1.0e30


class Sched:
    ENGS = ("pe", "act", "dve", "pool", "sp")

    def __init__(self, nc):
        self.nc = nc
        self.ops = {e: [] for e in self.ENGS}
        self.dcnt = {}
        self.lastw = {}
        self.readers = {}
        self.waited = {}
        self.same_engine_sync = True
        self.regions = {}
        self.keyreg = {}

    def alias(self, key, region, lo, hi):
        self.regions.setdefault(region, []).append((key, lo, hi))
        self.keyreg[key] = (region, lo, hi)

    def op(self, eng, fn, r=(), w=(), dma=None):
        deps = []
        for k in r:
            t = self.lastw.get(k)
            if t is not None:
                deps.append(t)
        for k in w:
            t = self.lastw.get(k)
            if t is not None:
                deps.append(t)
            deps.extend(self.readers.get(k, ()))
            if k in self.keyreg:
                reg, lo, hi = self.keyreg[k]
                for (k2, lo2, hi2) in self.regions[reg]:
                    if k2 != k and lo < hi2 and lo2 < hi:
                        t = self.lastw.get(k2)
                        if t is not None:
                            deps.append(t)
                        deps.extend(self.readers.get(k2, ()))
        waits = {}
        for (s, v, e) in deps:
            if e == eng and (eng == "pe" or not self.same_engine_sync):
                continue
            if self.waited.get((eng, s), 0) >= v:
                continue
            if waits.get(s, 0) < v:
                waits[s] = v
        for s, v in waits.items():
            self.waited[(eng, s)] = v
        if dma is None:
            s = "E_" + eng
            self.dcnt[s] = self.dcnt.get(s, 0) + 1
            v = self.dcnt[s]
            tok = (s, v, eng)
            self.ops[eng].append([fn, waits, None, v])
        else:
            s = "D_" + dma
            self.dcnt[s] = self.dcnt.get(s, 0) + 16
            v = self.dcnt[s]
            tok = (s, v, "dma")
            self.ops[eng].append([fn, waits, s, v])
        for k in r:
            self.readers.setdefault(k, []).append(tok)
        for k in w:
            self.lastw[k] = tok
            self.readers[k] = []
        return tok

    def emit(self, final_waits):
        nc = self.nc
        needed = {e: set() for e in self.ENGS}
        for e in self.ENGS:
            for (fn, waits, ds, v) in self.ops[e]:
                for s, wv in waits.items():
                    if s.startswith("E_"):
                        needed[s[2:]].add(wv)
        rank = {}
        for e in self.ENGS:
            for i, v in enumerate(sorted(needed[e])):
                rank[(e, v)] = i + 1
        names = set()
        for e in self.ENGS:
            if needed[e]:
                names.add("E_" + e)
            for (fn, waits, ds, v) in self.ops[e]:
                if ds is not None:
                    names.add(ds)
        with ExitStack() as st:
            sems = {n: st.enter_context(nc.semaphore(n)) for n in sorted(names)}
            block = st.enter_context(nc.Block())

            def run(eng_name):
                def body(e):
                    for (fn, waits, ds, v) in self.ops[eng_name]:
                        for s, wv in waits.items():
                            if s.startswith("E_"):
                                e.wait_ge(sems[s], rank[(s[2:], wv)])
                            else:
                                e.wait_ge(sems[s], wv)
                        ins = fn(e)
                        if ds is not None:
                            ins.then_inc(sems[ds], 16)
                        elif v in needed[eng_name]:
                            ins.then_inc(sems["E_" + eng_name], 1)
                    if eng_name == "sp":
                        for s in final_waits:
                            e.wait_ge(sems["D_" + s], self.dcnt["D_" + s])
                return body

            block.tensor(run("pe"))
            block.scalar(run("act"))
            block.vector(run("dve"))
            block.gpsimd(run("pool"))
            block.sync(run("sp"))


def build(nb_pre, nb_own, dbg=(), stop_after=None):
    nc = bass.Bass("TRN2", target_bir_lowering=False)
    ntok = nb_own * TB
    npre = max(nb_pre, 1) * TB

    def din(name, shape):
        return nc.dram_tensor(name, list(shape), F32, kind="ExternalInput").ap()

    xo = din("xo", [ntok, D])
    xp = din("xp", [npre, D])
    flag_d = din("flag", [P, 1])
    g1T_d = din("g1T", [P, KD])
    g2T_d = din("g2T", [P, KD])
    gmixT_d = din("gmixT", [P, KD])
    gf_d = din("gf", [D])
    gvn_d = din("gvn", [1024])
    w_in_d = din("w_in", [D, PROJ_W])
    wsT_d = din("wsT", [P, 8, P])
    bsp_d = din("bsp", [P, 8])
    cw_d = din("cw", [P, 8, 4])
    cb_d = din("cb", [P, 8])
    bi_d = din("bi", [4])
    bf_d = din("bf", [4])
    w_out_d = din("w_out", [D, D])
    wq_d = din("wq", [D, 1024])
    kT12_d = din("kT12", [P, 8, 256])
    UT_d = din("UT", [P, KD, NEXP])
    V_d = din("V", [NEXP, D])
    ident_d = din("ident", [P, P])
    maskT_d = din("maskT", [P, P])
    y_d = nc.dram_tensor("y", [ntok, D], F32, kind="ExternalOutput").ap()
    dbg_out = {}

    w_in_v = w_in_d.rearrange("(k p) n -> p k n", p=P)
    w_out_v = w_out_d.rearrange("(k p) n -> p k n", p=P)
    wq_v = wq_d.rearrange("(k p) n -> p k n", p=P)
    V_v = V_d.rearrange("(c p) d -> p c d", p=P)

    S = Sched(nc)
    es = ExitStack()

    def sb(name, shape, dt):
        return es.enter_context(nc.sbuf_tensor("s_" + name, list(shape), dt))

    def ps(name, shape, dt=F32):
        return es.enter_context(nc.psum_tensor("p_" + name, list(shape), dt))

    xres = sb("xres", [P, NT, D], F32)
    Hreg = sb("Hreg", [P, KD, TB], BF16)
    Wring = [sb(f"W{i}", [P, KD, 512], BF16) for i in range(3)]
    Yreg = sb("Yreg", [P, NT * D], BF16)
    Greg = sb("Greg", [P, 12288], BF16)
    ident = sb("ident", [P, P], BF16)
    maskT = sb("maskT", [P, P], BF16)
    tri32 = sb("tri32", [P, P], F32)
    ones32 = sb("ones32", [P, P], F32)
    g1T = sb("g1T", [P, KD], F32)
    g2T = sb("g2T", [P, KD], F32)
    gmixT = sb("gmixT", [P, KD], F32)
    gvn_rep = sb("gvn_rep", [P, 1024], F32)
    wsT = sb("wsT", [P, 8, P], BF16)
    bsp = sb("bsp", [P, 8], F32)
    cw = sb("cw", [P, 8, 4], F32)
    cb = sb("cb", [P, 8], F32)
    bi_rep = sb("bi_rep", [P, 4], F32)
    bf_rep = sb("bf_rep", [P, 4], F32)
    kT12 = sb("kT12", [P, 8, 256], BF16)
    flag = sb("flag", [P, 1], F32)
    Cst = sb("Cst", [P, 4, 257], F32)
    hist = sb("hist", [P, 8, 3], F32)
    xn = [sb(f"xn{i}", [P, D], BF16) for i in range(2)]
    junk = xn[1]
    ctmp = [sb(f"ctmp{i}", [P, 512], F32) for i in range(2)]
    sg = [sb(f"sg{i}", [P, 512], F32) for i in range(2)]
    qT = sb("qT", [P, 4, TB], BF16)
    kT = sb("kT", [P, 4, TB], BF16)
    ktok = sb("ktok", [P, NT, 4, P], BF16)
    tmpf = sb("tmpf", [P, 1024], F32)
    sTm = sb("sTm", [P, 4, P], BF16)
    Cs_bf = sb("Cs_bf", [P, 4, 257], BF16)
    Cs_f = sb("Cs_f", [P, 4, 257], F32)
    ssq = sb("ssq", [P, 16], F32)
    rstd = sb("rstd", [P, 16], F32)
    if_sb = sb("if_sb", [P, NT, 8], F32)
    gsm = sb("gsm", [P, NT, 32], F32)
    hsm = sb("hsm", [P, 32], F32)

    tv = sb("tv", [P, 32], F32)
    svall = sb("svall", [P, 8, 24], F32)
    psc = sb("psc", [P, NT, 8, 4], F32)
    pst = sb("pst", [P, 8, 24], F32)
    S12 = []
    for i in range(NT):
        base = Yreg[:, i * 4096:(i + 1) * 4096] if i < 2 else Greg[:, (i - 2) * 4096:(i - 1) * 4096]
        S12.append(base.bitcast(F32).rearrange("p (h c) -> p h c", h=8))
    qTp = Greg[:, 8192:12288].rearrange("p (h c) -> p h c", h=8)
    zb = ctmp
    Eb = [sg[j][:].bitcast(BF16)[:, 0:512] for j in range(2)]
    Gh = [sg[j][:].bitcast(BF16)[:, 512:1024] for j in range(2)]
    ga = [xn[0][:, j * 512:(j + 1) * 512] for j in range(4)]
    gax = [sb(f"gax{j}", [P, 512], BF16) for j in range(2)]
    GA = [gax[0][:], gax[1][:]]
    Gacc = [xn[1][:, j * 512:(j + 1) * 512] for j in range(2)]
    GAT = [xn[1][:, 1024 + j * 512:1024 + (j + 1) * 512].rearrange("p (j c) -> p j c", j=4) for j in range(2)]
    ghx = [sb(f"ghx{j}", [P, 512], BF16) for j in range(2)]
    zbx = [sb(f"zbx{j}", [P, 512], F32) for j in range(2)]
    ebx = [sb(f"ebx{j}", [P, 512], BF16) for j in range(2)]
    zb4 = [ctmp[0][:], ctmp[1][:], zbx[0][:], zbx[1][:]]
    zk4 = ["ctmp0", "ctmp1", "zbx0", "zbx1"]
    Eb4 = [Eb[0], Eb[1], ebx[0][:], ebx[1][:]]
    ek4 = ["Eb0", "Eb1", "ebx0", "ebx1"]
    NZ = 4
    Ghr = [Gh[0], Gh[1], ghx[0][:], ghx[1][:]]
    NGH = 4
    for j in range(2):
        S.alias(f"Ghr{j}", "SG%d" % j, 1024, 2048)
    cand = [tmpf[:, j * 256:(j + 1) * 256] for j in range(3)]
    wkb = tmpf[:, 768:896]
    for i in range(NT):
        if i < 2:
            S.alias(f"S12_{i}", "Y", i * 8192, (i + 1) * 8192)
        else:
            S.alias(f"S12_{i}", "G", (i - 2) * 8192, (i - 1) * 8192)
    S.alias("qTp", "G", 16384, 24576)
    S.alias("xn0", "XN0", 0, 4096)
    S.alias("xn1", "XN1", 0, 4096)
    S.alias("junk", "XN1", 0, 4096)
    for j in range(2):
        S.alias(f"ga{j}", "XN0", j * 1024, (j + 1) * 1024)
        S.alias(f"ga{j + 2}", "XN0", (j + 2) * 1024, (j + 3) * 1024)
        S.alias(f"Gacc{j}", "XN1", j * 1024, (j + 1) * 1024)
        S.alias(f"GAT{j}", "XN1", 2048 + j * 1024, 2048 + (j + 1) * 1024)
        S.alias(f"sg{j}", "SG%d" % j, 0, 2048)
        S.alias(f"Eb{j}", "SG%d" % j, 0, 1024)
        S.alias(f"Gh{j}", "SG%d" % j, 1024, 2048)

    G32 = Greg[:, 0:8192].bitcast(F32).rearrange("p (i c) -> p i c", i=NT)
    vg = Greg[:, 8192:12288].rearrange("p (i c) -> p i c", i=NT)
    vaug = Greg[:, 0:4112].rearrange("p (i h c) -> p i h c", i=NT, h=4)
    so = Greg[:, 4224:8320].rearrange("p (i c) -> p i c", i=NT)
    pre = [Greg[:, 8448 + j * 1040: 8448 + j * 1040 + 1030].bitcast(F32) for j in range(2)]
    ymix = Yreg[:].rearrange("p (i c) -> p i c", i=NT)

    for i in range(NT):
        S.alias(f"G32_{i}", "G", i * 4096, (i + 1) * 4096)
        S.alias(f"vg{i}", "G", 16384 + i * 2048, 16384 + (i + 1) * 2048)
        S.alias(f"vaug{i}", "G", i * 2056, (i + 1) * 2056)
        S.alias(f"so{i}", "G", 8448 + i * 2048, 8448 + (i + 1) * 2048)
        S.alias(f"ymix{i}", "Y", i * 4096, (i + 1) * 4096)
    for j in range(2):
        S.alias(f"pre{j}", "G", 16896 + j * 2080, 16896 + j * 2080 + 2060)

    PTt = ps("PT", [P, 1024])
    PPt = ps("PP", [P, 1024])
    PBt = ps("PB", [P, 2048])
    PT = PTt[:].bitcast(BF16).rearrange("p (k c) -> p k c", k=KD)
    PP = [PPt[:, 0:512], PPt[:, 512:1024]]
    PM = PBt[:, 0:1024]
    PS = PBt[:, 1024:1536]
    PX = PBt[:, 1536:2048]
    PXbf = PBt[:, 1536 + 128:1536 + 384].bitcast(BF16).rearrange("p (h c) -> p h c", h=4)

    cnt = {"pp": 0, "w": 0, "xn": 0, "pre": 0, "ct": 0, "ga": 0}

    def nxt(name, n):
        v = cnt[name] % n
        cnt[name] += 1
        return v

    def ld(eng, out, in_, key, slot):
        S.op(eng, lambda e: e.dma_start(out=out, in_=in_), r=(), w=(key,), dma="c_" + key)

    ld("pool", ident[:], ident_d, "ident", "c0")
    ld("pool", maskT[:], maskT_d, "maskT", "c0")
    ld("pool", wsT[:], wsT_d, "wsT", "c0")
    ld("pool", kT12[:], kT12_d, "kT12", "c0")
    ld("sp", tri32[:], maskT_d, "tri32", "c1")
    ld("sp", g1T[:], g1T_d, "g1T", "c1")
    ld("sp", g2T[:], g2T_d, "g2T", "c1")
    ld("sp", gmixT[:], gmixT_d, "gmixT", "c1")
    ld("sp", gvn_rep[:], gvn_d.partition_broadcast(P), "gvn_rep", "c1")
    ld("sp", bsp[:], bsp_d, "bsp", "c1")
    ld("sp", cw[:], cw_d, "cw", "c1")
    ld("sp", cb[:], cb_d, "cb", "c1")
    ld("sp", bi_rep[:], bi_d.partition_broadcast(P), "bi_rep", "c1")
    ld("sp", bf_rep[:], bf_d.partition_broadcast(P), "bf_rep", "c1")
    ld("sp", flag[:], flag_d, "flag", "c1")
    S.op("dve", lambda e: e.memset(ones32[:], 1.0), w=("ones32",))
    S.op("dve", lambda e: e.memset(Cst[:], 0.0), w=("Cst",))
    S.op("dve", lambda e: e.memset(hist[:], 0.0), w=("hist",))
    S.op("dve", lambda e: e.tensor_tensor(out=wsT[:], in0=wsT[:], in1=maskT[:].unsqueeze(1).to_broadcast([P, 8, P]), op=ALU.mult),
         r=("wsT", "maskT"), w=("wsT",))

    def wload(src_ap, ncols=512):
        slot = nxt("w", 3)
        wt = Wring[slot]
        dst = wt[:, :, 0:ncols]
        S.op("pool", lambda e: e.dma_start(out=dst, in_=src_ap), w=(f"W{slot}",), dma=f"W{slot}")
        return wt, f"W{slot}"

    def rms_rstd(src_ap, key_src, col, n):
        S.op("act", lambda e: e.activation(out=junk[:, 0:n], in_=src_ap, func=AF.Square, accum_out=ssq[:, col:col + 1]),
             r=(key_src,), w=("junk", f"ssq{col}"))
        S.op("act", lambda e: e.activation(out=ssq[:, col:col + 1], in_=ssq[:, col:col + 1], func=AF.Sqrt, scale=1.0 / n, bias=eps_t[:, 0:1]),
             r=(f"ssq{col}", "eps"), w=(f"ssq{col}",))
        S.op("dve", lambda e: e.reciprocal(out=rstd[:, col:col + 1], in_=ssq[:, col:col + 1]), r=(f"ssq{col}",), w=(f"rstd{col}",))

    eps_t = sb("eps_t", [P, 1], F32)
    S.op("dve", lambda e: e.memset(eps_t[:], EPS), w=("eps",))

    def norm_transpose(i, src_ap, key_src, gT, gkey, dst_keyname):
        col = i
        rms_rstd(src_ap, key_src, col, D)
        xs = nxt("xn", 2)
        xnb = xn[xs]
        S.op("dve", lambda e: e.tensor_scalar(out=xnb[:], in0=src_ap, scalar1=rstd[:, col:col + 1], scalar2=None, op0=ALU.mult),
             r=(key_src, f"rstd{col}"), w=(f"xn{xs}",))
        for k in range(KD):
            S.op("pe", lambda e, k=k: e.transpose(out=PT[:, k, :], in_=xnb[:, k * P:(k + 1) * P], identity=ident[:]),
                 r=(f"xn{xs}", "ident"), w=("PT0", "PT1"))
        S.op("dve", lambda e: e.tensor_tensor(out=Hreg[:, :, i * P:(i + 1) * P], in0=PT, in1=gT[:].unsqueeze(2).to_broadcast([P, KD, P]), op=ALU.mult),
             r=("PT0", "PT1", gkey), w=(f"{dst_keyname}{i}",))

    def proj_tok(i, hkey, wt, wkey, ncols):
        s = nxt("pp", 2)
        out = PP[s][:, 0:ncols]
        for k in range(KD):
            S.op("pe", lambda e, k=k: e.matmul(out, lhsT=Hreg[:, k, i * P:(i + 1) * P], rhs=wt[:, k, 0:ncols], start=(k == 0), stop=(k == KD - 1)),
                 r=(f"{hkey}{i}", wkey), w=(f"PP{s}",))
        return PP[s], f"PP{s}"

    def proj_feat(hkey, wt, wkey, c0):
        s = nxt("pp", 2)
        out = PP[s]
        for k in range(KD):
            S.op("pe", lambda e, k=k: e.matmul(out, lhsT=wt[:, k, c0:c0 + P], rhs=Hreg[:, k, :], start=(k == 0), stop=(k == KD - 1)),
                 r=tuple(f"{hkey}{i}" for i in range(NT)) + (wkey,), w=(f"PP{s}",))
        return PP[s], f"PP{s}"

    def dbg_dump(name, ap, shape, key):
        if name not in dbg:
            return
        t = nc.dram_tensor("dbg_" + name, list(shape), F32, kind="ExternalOutput").ap()
        dbg_out[name] = t
        S.op("sp", lambda e: e.dma_start(out=t, in_=ap), r=key, dma="dbg")

    def phaseA(blk, is_pre, last_pre):
        xsrc = xp if is_pre else xo
        b0 = (blk if is_pre else blk - nb_pre) * TB
        for i in range(NT):
            S.op("sp", lambda e, i=i: e.dma_start(out=xres[:, i, :], in_=xsrc[b0 + i * P: b0 + (i + 1) * P, :]),
                 w=(f"xres{i}",), dma=f"x{i}")
            norm_transpose(i, xres[:, i, :], f"xres{i}", g1T, "g1T", "hT")
        if "hT" in dbg and not is_pre and blk == nb_pre:
            pass

        wt, wk_ = wload(w_in_v[:, :, 5120:5128], 8)
        for i in range(NT):
            pp, pk = proj_tok(i, "hT", wt, wk_, 8)
            S.op("act", lambda e, i=i, pp=pp: e.copy(out=if_sb[:, i, :], in_=pp[:, 0:8]), r=(pk,), w=(f"if{i}",))
            g = gsm[:, i, :]
            S.op("dve", lambda e, i=i, g=g: e.tensor_tensor(out=g[:, 20:24], in0=if_sb[:, i, 4:8], in1=bf_rep[:], op=ALU.add),
                 r=(f"if{i}", "bf_rep"), w=(f"g{i}",))
            S.op("act", lambda e, g=g: e.activation(out=g[:, 20:24], in_=g[:, 20:24], func=AF.Exp, scale=-1.0), r=(f"g{i}",), w=(f"g{i}",))
            S.op("act", lambda e, g=g: e.activation(out=g[:, 0:4], in_=g[:, 20:24], func=AF.Ln, bias=one_t[:, 0:1]), r=(f"g{i}", "one"), w=(f"g{i}",))
            S.op("pe", lambda e, g=g: e.matmul(PX[:, 0:4], lhsT=tri32[:], rhs=g[:, 0:4], start=True, stop=True), r=(f"g{i}", "tri32"), w=("PX",))
            S.op("pe", lambda e, g=g: e.matmul(PX[:, 4:8], lhsT=ones32[:], rhs=g[:, 0:4], start=True, stop=True), r=(f"g{i}", "ones32"), w=("PX",))
            S.op("act", lambda e, g=g: e.copy(out=g[:, 24:32], in_=PX[:, 0:8]), r=("PX",), w=(f"g{i}",))
            S.op("dve", lambda e, g=g: e.tensor_tensor(out=g[:, 4:8], in0=g[:, 24:28], in1=g[:, 28:32], op=ALU.subtract), r=(f"g{i}",), w=(f"g{i}",))
            S.op("dve", lambda e, i=i, g=g: e.tensor_tensor(out=g[:, 20:24], in0=if_sb[:, i, 0:4], in1=bi_rep[:], op=ALU.add), r=(f"if{i}", "bi_rep", f"g{i}"), w=(f"g{i}",))
            S.op("dve", lambda e, g=g: e.tensor_tensor(out=g[:, 20:24], in0=g[:, 20:24], in1=g[:, 4:8], op=ALU.add), r=(f"g{i}",), w=(f"g{i}",))
            S.op("act", lambda e, g=g: e.activation(out=g[:, 8:12], in_=g[:, 20:24], func=AF.Exp), r=(f"g{i}",), w=(f"g{i}",))
            S.op("act", lambda e, g=g: e.activation(out=g[:, 12:16], in_=g[:, 4:8], func=AF.Exp), r=(f"g{i}",), w=(f"g{i}",))
            S.op("act", lambda e, g=g: e.activation(out=g[:, 16:20], in_=g[:, 28:32], func=AF.Exp, scale=-1.0), r=(f"g{i}",), w=(f"g{i}",))

        if not is_pre:
            for half in range(2):
                wt, wk_ = wload(w_in_v[:, :, 1024 + half * 512: 1024 + (half + 1) * 512])
                for i in range(NT):
                    pp, pk = proj_tok(i, "hT", wt, wk_, 512)
                    S.op("act", lambda e, i=i, pp=pp, half=half: e.activation(out=G32[:, i, half * 512:(half + 1) * 512], in_=pp, func=AF.Gelu_apprx_tanh),
                         r=(pk,), w=(f"G32_{i}",))
            for i in range(NT):
                rms_rstd(G32[:, i, :], f"G32_{i}", 4 + i, 1024)
                S.op("dve", lambda e, i=i: e.scalar_tensor_tensor(out=vg[:, i, :], in0=G32[:, i, :], scalar=rstd[:, 4 + i:5 + i], in1=gvn_rep[:], op0=ALU.mult, op1=ALU.mult),
                     r=(f"G32_{i}", f"rstd{4 + i}", "gvn_rep"), w=(f"vg{i}",))
            for i in range(NT):
                for h in range(8):
                    S.op("pe", lambda e, i=i, h=h: e.matmul(PM[:, h * P:(h + 1) * P], lhsT=wsT[:, h, :], rhs=vg[:, i, h * P:(h + 1) * P], start=True, stop=True),
                         r=("wsT", f"vg{i}"), w=("PM",))
                S.op("dve", lambda e, i=i: e.tensor_tensor(out=G32[:, i, :].rearrange("p (h c) -> p h c", h=8), in0=PM.rearrange("p (h c) -> p h c", h=8),
                                                           in1=bsp[:].unsqueeze(2).to_broadcast([P, 8, P]), op=ALU.add),
                     r=("PM", "bsp", f"vg{i}"), w=(f"G32_{i}",))
            for half in range(2):
                wt, wk_ = wload(w_in_v[:, :, half * 512:(half + 1) * 512])
                for i in range(NT):
                    pp, pk = proj_tok(i, "hT", wt, wk_, 512)
                    c = nxt("ct", 2)
                    S.op("act", lambda e, pp=pp, c=c: e.activation(out=ctmp[c][:], in_=pp, func=AF.Gelu_apprx_tanh), r=(pk,), w=(f"ctmp{c}",))
                    S.op("dve", lambda e, i=i, half=half, c=c: e.tensor_tensor(out=G32[:, i, half * 512:(half + 1) * 512], in0=G32[:, i, half * 512:(half + 1) * 512], in1=ctmp[c][:], op=ALU.mult),
                         r=(f"ctmp{c}", f"G32_{i}"), w=(f"G32_{i}",))
            for i in range(NT):
                for h in range(8):
                    S.op("act", lambda e, i=i, h=h: e.activation(out=junk[:, 0:P], in_=G32[:, i, h * P:(h + 1) * P], func=AF.Square, accum_out=hsm[:, h:h + 1]),
                         r=(f"G32_{i}",), w=("junk", "hsm"))
                S.op("act", lambda e: e.activation(out=hsm[:, 0:8], in_=hsm[:, 0:8], func=AF.Sqrt, scale=1.0 / P, bias=eps_t[:, 0:1]), r=("hsm", "eps"), w=("hsm",))
                S.op("dve", lambda e: e.reciprocal(out=hsm[:, 8:16], in_=hsm[:, 0:8]), r=("hsm",), w=("hsm",))
                S.op("dve", lambda e, i=i: e.tensor_tensor(out=ymix[:, i, 0:1024].rearrange("p (h c) -> p h c", h=8), in0=G32[:, i, :].rearrange("p (h c) -> p h c", h=8),
                                                           in1=hsm[:, 8:16].unsqueeze(2).to_broadcast([P, 8, P]), op=ALU.mult),
                     r=(f"G32_{i}", "hsm"), w=(f"ymix{i}",))

        chunks = list(range(8)) if (not is_pre or last_pre) else list(range(4, 8))
        for part in ((0, 1) if (not is_pre or last_pre) else (1,)):
            wt, wk_ = wload(w_in_v[:, :, 2048 + part * 512: 2048 + (part + 1) * 512])
            for jj in range(4):
                j = part * 4 + jj
                pp, pk = proj_feat("hT", wt, wk_, jj * P)
                pr = nxt("pre", 2)
                pb = pre[pr]
                S.op("act", lambda e, pb=pb, pp=pp: e.copy(out=pb[:, 3:515], in_=pp), r=(pk,), w=(f"pre{pr}",))
                S.op("dve", lambda e, pb=pb, j=j: e.tensor_copy(out=pb[:, 0:3], in_=hist[:, j, :]), r=("hist",), w=(f"pre{pr}",))
                S.op("dve", lambda e, pb=pb, j=j: e.tensor_copy(out=hist[:, j, :], in_=pb[:, 512:515]), r=(f"pre{pr}",), w=("hist",))
                if is_pre and part == 0:
                    continue
                c = nxt("ct", 2)
                ct = ctmp[c]
                S.op("dve", lambda e, pb=pb, j=j, ct=ct: e.tensor_scalar(out=ct[:], in0=pb[:, 0:512], scalar1=cw[:, j, 0:1], scalar2=cb[:, j:j + 1], op0=ALU.mult, op1=ALU.add),
                     r=(f"pre{pr}", "cw", "cb"), w=(f"ctmp{c}",))
                for tap in range(1, 4):
                    S.op("dve", lambda e, pb=pb, j=j, ct=ct, tap=tap: e.scalar_tensor_tensor(out=ct[:], in0=pb[:, tap:tap + 512], scalar=cw[:, j, tap:tap + 1], in1=ct[:], op0=ALU.mult, op1=ALU.add),
                         r=(f"pre{pr}", "cw", f"ctmp{c}"), w=(f"ctmp{c}",))
                S.op("act", lambda e, ct=ct, c=c: e.activation(out=sg[c][:], in_=ct[:], func=AF.Sigmoid), r=(f"ctmp{c}",), w=(f"sg{c}",))
                if part == 0:
                    S.op("dve", lambda e, ct=ct, c=c, jj=jj: e.tensor_tensor(out=qT[:, jj, :], in0=ct[:], in1=sg[c][:], op=ALU.mult),
                         r=(f"ctmp{c}", f"sg{c}"), w=("qT",))
                else:
                    S.op("dve", lambda e, ct=ct, c=c, jj=jj: e.scalar_tensor_tensor(out=kT[:, jj, :], in0=ct[:], scalar=float(P ** -0.5), in1=sg[c][:], op0=ALU.mult, op1=ALU.mult),
                         r=(f"ctmp{c}", f"sg{c}"), w=("kT",))

        for half in range(2):
            wt, wk_ = wload(w_in_v[:, :, 3072 + half * 512: 3072 + (half + 1) * 512])
            for i in range(NT):
                pp, pk = proj_tok(i, "hT", wt, wk_, 512)
                S.op("dve", lambda e, i=i, half=half, pp=pp: e.tensor_tensor(out=vaug[:, i, 2 * half:2 * half + 2, 0:256], in0=pp.rearrange("p (h c) -> p h c", h=2),
                                                                          in1=gsm[:, i, 8 + 2 * half:10 + 2 * half].unsqueeze(2).to_broadcast([P, 2, 256]), op=ALU.mult),
                     r=(pk, f"g{i}"), w=(f"vaug{i}",))
        for i in range(NT):
            S.op("dve", lambda e, i=i: e.tensor_copy(out=vaug[:, i, :, 256:257], in_=gsm[:, i, 8:12].unsqueeze(2)), r=(f"g{i}",), w=(f"vaug{i}",))
        if not is_pre:
            for half in range(2):
                wt, wk_ = wload(w_in_v[:, :, 4096 + half * 512: 4096 + (half + 1) * 512])
                for i in range(NT):
                    pp, pk = proj_tok(i, "hT", wt, wk_, 512)
                    S.op("act", lambda e, i=i, half=half, pp=pp: e.activation(out=so[:, i, half * 512:(half + 1) * 512], in_=pp, func=AF.Sigmoid), r=(pk,), w=(f"so{i}",))

        for i in range(NT):
            tsl = slice(i * P, (i + 1) * P)
            g = gsm[:, i, :]
            for h in range(4):
                S.op("pe", lambda e, h=h, tsl=tsl: e.transpose(out=PXbf[:, h, :], in_=kT[:, h, tsl], identity=ident[:]), r=("kT", "ident"), w=("PX",))
            S.op("act", lambda e, i=i: e.copy(out=ktok[:, i, :, :], in_=PXbf), r=("PX",), w=(f"ktok{i}",))
            S.op("dve", lambda e, g=g: e.tensor_tensor(out=Cs_f[:], in0=Cst[:], in1=g[:, 16:20].unsqueeze(2).to_broadcast([P, 4, 257]), op=ALU.mult),
                 r=("Cst", f"g{i}"), w=("Cs_f",))
            if not is_pre:
                S.op("act", lambda e: e.copy(out=Cs_bf[:], in_=Cs_f[:]), r=("Cs_f",), w=("Cs_bf",))
                for h in range(4):
                    S.op("pe", lambda e, h=h, tsl=tsl: e.matmul(PS[:, h * P:(h + 1) * P], lhsT=kT[:, h, tsl], rhs=qT[:, h, tsl], start=True, stop=True),
                         r=("kT", "qT"), w=("PS",))
                S.op("dve", lambda e: e.tensor_tensor(out=sTm[:], in0=PS.rearrange("p (h c) -> p h c", h=4), in1=maskT[:].unsqueeze(1).to_broadcast([P, 4, P]), op=ALU.mult),
                     r=("PS", "maskT"), w=("sTm",))
                for h in range(4):
                    S.op("pe", lambda e, h=h, i=i: e.matmul(PM[:, h * 256:(h + 1) * 256], lhsT=sTm[:, h, :], rhs=vaug[:, i, h, 0:256], start=True, stop=False),
                         r=("sTm", f"vaug{i}"), w=("PM",))
                    S.op("pe", lambda e, h=h, tsl=tsl: e.matmul(PM[:, h * 256:(h + 1) * 256], lhsT=qT[:, h, tsl], rhs=Cs_bf[:, h, 0:256], start=False, stop=True),
                         r=("qT", "Cs_bf"), w=("PM",))
                    S.op("pe", lambda e, h=h, i=i: e.matmul(PX[:, 8 + h:9 + h], lhsT=sTm[:, h, :], rhs=vaug[:, i, h, 256:257], start=True, stop=False),
                         r=("sTm", f"vaug{i}"), w=("PX",))
                    S.op("pe", lambda e, h=h, tsl=tsl: e.matmul(PX[:, 8 + h:9 + h], lhsT=qT[:, h, tsl], rhs=Cs_bf[:, h, 256:257], start=False, stop=True),
                         r=("qT", "Cs_bf"), w=("PX",))
            PPC = PPt[:].rearrange("p (h c) -> p h c", h=4)
            for h in range(4):
                S.op("pe", lambda e, h=h, i=i: e.matmul(PPt[:, h * 256:(h + 1) * 256], lhsT=ktok[:, i, h, :], rhs=vaug[:, i, h, 0:256], start=True, stop=True),
                     r=(f"ktok{i}", f"vaug{i}"), w=("PP0", "PP1"))
                S.op("pe", lambda e, h=h, i=i: e.matmul(PX[:, 12 + h:13 + h], lhsT=ktok[:, i, h, :], rhs=vaug[:, i, h, 256:257], start=True, stop=True),
                     r=(f"ktok{i}", f"vaug{i}"), w=("PX",))
            S.op("dve", lambda e: e.tensor_tensor(out=Cst[:, :, 0:256], in0=Cs_f[:, :, 0:256], in1=PPC, op=ALU.add), r=("Cs_f", "PP0", "PP1"), w=("Cst",))
            S.op("dve", lambda e: e.tensor_tensor(out=Cst[:, :, 256:257], in0=Cs_f[:, :, 256:257], in1=PX[:, 12:16].unsqueeze(2), op=ALU.add), r=("Cs_f", "PX"), w=("Cst",))
            if is_pre:
                continue
            S.op("act", lambda e: e.activation(out=hsm[:, 16:20], in_=PX[:, 8:12], func=AF.Abs), r=("PX",), w=("hsm2",))
            S.op("dve", lambda e, g=g: e.tensor_tensor(out=hsm[:, 16:20], in0=hsm[:, 16:20], in1=g[:, 12:16], op=ALU.max), r=("hsm2", f"g{i}"), w=("hsm2",))
            S.op("dve", lambda e: e.reciprocal(out=hsm[:, 20:24], in_=hsm[:, 16:20]), r=("hsm2",), w=("hsm2",))
            S.op("dve", lambda e: e.tensor_tensor(out=tmpf[:].rearrange("p (h c) -> p h c", h=4), in0=PM.rearrange("p (h c) -> p h c", h=4),
                                                  in1=hsm[:, 20:24].unsqueeze(2).to_broadcast([P, 4, 256]), op=ALU.mult), r=("PM", "hsm2"), w=("tmpf",))
            S.op("dve", lambda e, i=i: e.tensor_tensor(out=tmpf[:], in0=tmpf[:], in1=so[:, i, :], op=ALU.mult), r=("tmpf", f"so{i}"), w=("tmpf",))
            for h in range(4):
                S.op("act", lambda e, h=h: e.activation(out=junk[:, 0:256], in_=tmpf[:, h * 256:(h + 1) * 256], func=AF.Square, accum_out=hsm[:, 24 + h:25 + h]),
                     r=("tmpf",), w=("junk", "hsm3"))
            S.op("act", lambda e: e.activation(out=hsm[:, 24:28], in_=hsm[:, 24:28], func=AF.Sqrt, scale=1.0 / 256, bias=eps_t[:, 0:1]), r=("hsm3", "eps"), w=("hsm3",))
            S.op("dve", lambda e: e.reciprocal(out=hsm[:, 28:32], in_=hsm[:, 24:28]), r=("hsm3",), w=("hsm3",))
            S.op("dve", lambda e, i=i: e.tensor_tensor(out=ymix[:, i, 1024:2048].rearrange("p (h c) -> p h c", h=4), in0=tmpf[:].rearrange("p (h c) -> p h c", h=4),
                                                       in1=hsm[:, 28:32].unsqueeze(2).to_broadcast([P, 4, 256]), op=ALU.mult), r=("tmpf", "hsm3"), w=(f"ymix{i}",))
        if is_pre:
            return
        for i in range(NT):
            for k in range(KD):
                S.op("pe", lambda e, k=k, i=i: e.transpose(out=PT[:, k, :], in_=ymix[:, i, k * P:(k + 1) * P], identity=ident[:]),
                     r=(f"ymix{i}", "ident"), w=("PT0", "PT1"))
            S.op("dve", lambda e, i=i: e.tensor_tensor(out=Hreg[:, :, i * P:(i + 1) * P], in0=PT, in1=gmixT[:].unsqueeze(2).to_broadcast([P, KD, P]), op=ALU.mult),
                 r=("PT0", "PT1", "gmixT"), w=(f"hT{i}",))
        for gcol in range(4):
            wt, wk_ = wload(w_out_v[:, :, gcol * 512:(gcol + 1) * 512])
            for i in range(NT):
                pp, pk = proj_tok(i, "hT", wt, wk_, 512)
                S.op("dve", lambda e, i=i, gcol=gcol, pp=pp: e.tensor_tensor(out=xres[:, i, gcol * 512:(gcol + 1) * 512], in0=xres[:, i, gcol * 512:(gcol + 1) * 512], in1=pp, op=ALU.add),
                     r=(pk, f"xres{i}"), w=(f"xres{i}",))

    one_t = sb("one_t", [P, 1], F32)
    S.op("dve", lambda e: e.memset(one_t[:], 1.0), w=("one",))

    def phaseP(blk):
        for i in range(NT):
            norm_transpose(i, xres[:, i, :], f"xres{i}", g2T, "g2T", "hT")
        for part in range(2):
            wt, wk_ = wload(wq_v[:, :, part * 512:(part + 1) * 512])
            for hh in range(4):
                h = part * 4 + hh
                pp, pk = proj_feat("hT", wt, wk_, hh * P)
                S.op("act", lambda e, h=h, pp=pp: e.copy(out=qTp[:, h, :], in_=pp), r=(pk,), w=("qTp",))
        for i in range(NT):
            tsl = slice(i * P, (i + 1) * P)
            for hp in range(2):
                for hh in range(4):
                    h = hp * 4 + hh
                    S.op("pe", lambda e, h=h, hh=hh, tsl=tsl: e.matmul(PM[:, hh * 256:(hh + 1) * 256], lhsT=qTp[:, h, tsl], rhs=kT12[:, h, :], start=True, stop=True),
                         r=("qTp", "kT12"), w=("PM",))
                S.op("act", lambda e, i=i, hp=hp: e.copy(out=S12[i][:, hp * 4:(hp + 1) * 4, :], in_=PM.rearrange("p (h c) -> p h c", h=4)),
                     r=("PM",), w=(f"S12_{i}",))
        for i in range(NT):
            for h in range(8):
                for half in range(2):
                    src = S12[i][:, h, half * P:(half + 1) * P]
                    o = half * 16
                    S.op("dve", lambda e, src=src, o=o: e.max(out=tv[:, o:o + 8], in_=src), r=(f"S12_{i}",), w=("tv",))
                    S.op("dve", lambda e, src=src, o=o: e.match_replace(out=wkb, in_to_replace=tv[:, o:o + 8], in_values=src, imm_value=NEG),
                         r=(f"S12_{i}", "tv"), w=("tmpf",))
                    S.op("dve", lambda e, o=o: e.max(out=tv[:, o + 8:o + 16], in_=wkb), r=("tmpf",), w=("tv",))
                S.op("dve", lambda e: e.tensor_tensor(out=cand[0].rearrange("p (a b) -> p a b", a=16), in0=tv[:, 0:16].unsqueeze(2).to_broadcast([P, 16, 16]),
                                                      in1=tv[:, 16:32].unsqueeze(1).to_broadcast([P, 16, 16]), op=ALU.add), r=("tv",), w=("tmpf",))
                S.op("dve", lambda e, h=h: e.max(out=svall[:, h, 0:8], in_=cand[0]), r=("tmpf",), w=("svall",))
                S.op("dve", lambda e, h=h: e.match_replace(out=cand[1], in_to_replace=svall[:, h, 0:8], in_values=cand[0], imm_value=NEG), r=("tmpf", "svall"), w=("tmpf",))
                S.op("dve", lambda e, h=h: e.max(out=svall[:, h, 8:16], in_=cand[1]), r=("tmpf",), w=("svall",))
                S.op("dve", lambda e, h=h: e.match_replace(out=cand[2], in_to_replace=svall[:, h, 8:16], in_values=cand[1], imm_value=NEG), r=("tmpf", "svall"), w=("tmpf",))
                S.op("dve", lambda e, h=h: e.max(out=svall[:, h, 16:24], in_=cand[2]), r=("tmpf",), w=("svall",))
            S.op("dve", lambda e: e.tensor_tensor(out=pst[:, :, 0:16], in0=svall[:, :, 0:16], in1=svall[:, :, 0:1].to_broadcast([P, 8, 16]), op=ALU.subtract),
                 r=("svall",), w=("pst",))
            S.op("act", lambda e: e.activation(out=pst[:, :, 0:16], in_=pst[:, :, 0:16], func=AF.Exp), r=("pst",), w=("pst",))
            S.op("dve", lambda e: e.reduce_sum(out=pst[:, :, 16:17], in_=pst[:, :, 0:16], axis=AX.X), r=("pst",), w=("pst",))
            S.op("act", lambda e: e.activation(out=pst[:, :, 17:18], in_=pst[:, :, 16:17], func=AF.Ln), r=("pst",), w=("pst",))
            S.op("dve", lambda e, i=i: e.scalar_tensor_tensor(out=psc[:, i, :, 0:1], in0=svall[:, :, 0:1], scalar=-1.0, in1=pst[:, :, 17:18], op0=ALU.mult, op1=ALU.subtract),
                 r=("svall", "pst"), w=(f"psc{i}",))
            S.op("dve", lambda e: e.tensor_tensor(out=pst[:, :, 18:19], in0=svall[:, :, 15:16], in1=svall[:, :, 16:17], op=ALU.add), r=("svall", "pst"), w=("pst",))
            S.op("dve", lambda e, i=i: e.scalar_tensor_tensor(out=psc[:, i, :, 1:2], in0=pst[:, :, 18:19], scalar=0.5, in1=psc[:, i, :, 0:1], op0=ALU.mult, op1=ALU.add),
                 r=("pst", f"psc{i}"), w=(f"psc{i}",))
            S.op("dve", lambda e, i=i: e.tensor_scalar(out=psc[:, i, :, 2:3], in0=psc[:, i, :, 1:2], scalar1=-1.0, scalar2=None, op0=ALU.mult),
                 r=(f"psc{i}",), w=(f"psc{i}",))
            S.op("dve", lambda e, i=i: e.tensor_tensor(out=psc[:, i, :, 3:4], in0=psc[:, i, :, 0:1], in1=psc[:, i, :, 2:3], op=ALU.add),
                 r=(f"psc{i}",), w=(f"psc{i}",))
        units = [(g, i) for g in range(NG) for i in range(NT)]
        NU = len(units)
        wts = {}

        def load_u(g):
            wts[("u", g)] = wload(UT_d[:, :, g * GE:(g + 1) * GE])

        def load_v(g):
            slot = nxt("w", 3)
            vt = Wring[slot][:].rearrange("p k c -> p (k c)").rearrange("p (j d) -> p j d", j=GC)
            vk = f"W{slot}"
            S.op("pool", lambda e, vt=vt, g=g: e.dma_start(out=vt, in_=V_v[:, g * GC:(g + 1) * GC, :]), w=(vk,), dma=vk)
            wts[("v", g)] = (vt, vk)

        st = {}

        def stageA_pe(u):
            g, i = units[u]
            ut, uk = wts[("u", g)]
            sl = u % 2
            for k in range(KD):
                S.op("pe", lambda e, k=k, i=i, ut=ut, sl=sl: e.matmul(PP[sl], lhsT=Hreg[:, k, i * P:(i + 1) * P], rhs=ut[:, k, 0:GE], start=(k == 0), stop=(k == KD - 1)),
                     r=(f"hT{i}", uk), w=(f"PP{sl}",))

        def gelu_pair(kp):
            s0 = (2 * kp) % 4
            S.op("act", lambda e, s0=s0: e.activation(out=xn[0][:, s0 * 512:(s0 + 2) * 512], in_=PPt[:, 0:1024], func=AF.Gelu_apprx_tanh),
                 r=("PP0", "PP1"), w=(f"ga{s0}", f"ga{s0 + 1}"))

        DVEH = (1, 2)
        BIGA = 1.0e7

        def Z(n):
            u, h = divmod(n, 8)
            g, i = units[u]
            a = u % 2
            c = n % NZ
            S.op("dve", lambda e, i=i, h=h, c=c, g=g: e.scalar_tensor_tensor(
                out=zb4[c].rearrange("p (j c) -> p j c", j=GC),
                in0=S12[i][:, h, P:2 * P].unsqueeze(1).to_broadcast([P, GC, P]),
                scalar=psc[:, i, h, 3:4],
                in1=S12[i][:, h, g * GC:(g + 1) * GC].unsqueeze(2).to_broadcast([P, GC, P]),
                op0=ALU.add, op1=ALU.add), r=(f"S12_{i}", f"psc{i}"), w=(zk4[c],))
            if h not in DVEH:
                S.op("act", lambda e, i=i, h=h, c=c: e.activation(out=zb4[c], in_=zb4[c], func=AF.Prelu, alpha=BIGA),
                     r=(zk4[c],), w=(zk4[c],))
                if h == 0:
                    S.op("act", lambda e, i=i, h=h, c=c, a=a: e.activation(out=Gacc[a], in_=zb4[c], func=AF.Exp, bias=psc[:, i, h, 1:2]),
                         r=(zk4[c], f"psc{i}"), w=(f"Gacc{a}",))
                else:
                    S.op("act", lambda e, i=i, h=h, c=c: e.activation(out=Eb4[c], in_=zb4[c], func=AF.Exp, bias=psc[:, i, h, 1:2]),
                         r=(zk4[c], f"psc{i}"), w=(ek4[c],))
            else:
                S.op("act", lambda e, i=i, h=h, c=c: e.activation(out=Eb4[c], in_=zb4[c], func=AF.Exp, bias=psc[:, i, h, 1:2]), r=(zk4[c], f"psc{i}"), w=(ek4[c],))

        def M(n):
            u, h = divmod(n, 8)
            g, i = units[u]
            a = u % 2
            c = n % NZ
            if h == 0:
                return
            if h not in DVEH:
                S.op("dve", lambda e, c=c, a=a: e.tensor_tensor(out=Gacc[a], in0=Gacc[a], in1=Eb4[c], op=ALU.add), r=(f"Gacc{a}", ek4[c]), w=(f"Gacc{a}",))
            else:
                q = h % NGH
                S.op("dve", lambda e, i=i, h=h, c=c, q=q: e.scalar_tensor_tensor(out=Ghr[q], in0=zb4[c], scalar=0.0, in1=Eb4[c], op0=ALU.is_ge, op1=ALU.mult),
                     r=(zk4[c], ek4[c]), w=(f"Ghr{q}",))
                S.op("dve", lambda e, q=q, a=a: e.tensor_tensor(out=Gacc[a], in0=Gacc[a], in1=Ghr[q], op=ALU.add), r=(f"Gacc{a}", f"Ghr{q}"), w=(f"Gacc{a}",))

        def GAop(u):
            a = u % 2
            s4 = u % 4
            S.op("dve", lambda e, a=a, s4=s4: e.tensor_tensor(out=GA[a], in0=Gacc[a], in1=ga[s4], op=ALU.mult), r=(f"Gacc{a}", f"ga{s4}"), w=(f"GA{a}",))

        def stageC1(u):
            a = u % 2
            ptv = PT[:, 8 * a:8 * a + 4, :]
            for j in range(GC):
                S.op("pe", lambda e, a=a, j=j, ptv=ptv: e.transpose(out=ptv[:, j, :], in_=GA[a][:, j * P:(j + 1) * P], identity=ident[:]),
                     r=(f"GA{a}", "ident"), w=(f"PT{a}",))

        def stageC2(u):
            g, i = units[u]
            a = u % 2
            vt, vk = wts[("v", g)]
            ptv = PT[:, 8 * a:8 * a + 4, :]
            S.op("act", lambda e, a=a, ptv=ptv: e.copy(out=GAT[a], in_=ptv), r=(f"PT{a}",), w=(f"GAT{a}",))
            for dc in range(4):
                for j in range(GC):
                    S.op("pe", lambda e, a=a, j=j, dc=dc, vt=vt: e.matmul(PBt[:, dc * 512:(dc + 1) * 512], lhsT=GAT[a][:, j, :], rhs=vt[:, j, dc * 512:(dc + 1) * 512],
                                                                     start=(j == 0), stop=(j == GC - 1)),
                         r=(f"GAT{a}", vk), w=("PM", "PS", "PX"))

            def add():
                S.op("dve", lambda e, i=i: e.tensor_tensor(out=xres[:, i, :], in0=xres[:, i, :], in1=PBt[:], op=ALU.add), r=("PM", "PS", "PX", f"xres{i}"), w=(f"xres{i}",))
            return add

        def a_pe_with_loads(u):
            stageA_pe(u)
            g1, i1 = units[u]
            if i1 == NT - 1 and g1 + 1 < NG:
                load_v(g1 + 1)

        load_u(0)
        load_v(0)
        load_u(1)
        a_pe_with_loads(0)
        a_pe_with_loads(1)
        gelu_pair(0)
        a_pe_with_loads(2)
        a_pe_with_loads(3)
        LOOK = NZ - 1
        NH = 8 * NU
        for n in range(min(LOOK, NH)):
            Z(n)
        pending = None
        for n in range(NH):
            u, h = divmod(n, 8)
            if n + LOOK < NH:
                Z(n + LOOK)
            M(n)
            if h == 3 and u >= 1:
                if u % 2 == 0:
                    gelu_pair(u // 2)
                pending()
                pending = None
                gp, ip = units[u - 1]
                if ip == NT - 1 and gp + 2 < NG:
                    load_u(gp + 2)
                if u % 2 == 0:
                    for uu in (u + 2, u + 3):
                        if uu < NU:
                            a_pe_with_loads(uu)
            if h == 7:
                GAop(u)
                stageC1(u)
                pending = stageC2(u)
        pending()
        gslot = nxt("w", 3)
        gfv = Wring[gslot][:].rearrange("p k c -> p (k c)")[:, 0:2 * D].bitcast(F32)
        gfk = f"W{gslot}"
        S.op("pool", lambda e: e.dma_start(out=gfv, in_=gf_d.partition_broadcast(P)), w=(gfk,), dma=gfk)
        for i in range(NT):
            rms_rstd(xres[:, i, :], f"xres{i}", 8 + i, D)
            S.op("dve", lambda e, i=i: e.scalar_tensor_tensor(out=xres[:, i, :], in0=xres[:, i, :], scalar=rstd[:, 8 + i:9 + i], in1=gfv, op0=ALU.mult, op1=ALU.mult),
                 r=(f"xres{i}", f"rstd{8 + i}", gfk), w=(f"xres{i}",))

    def phaseOut(blk):
        b0 = (blk - nb_pre) * TB
        for i in range(NT):
            S.op("sp", lambda e, i=i: e.dma_start(out=y_d[b0 + i * P: b0 + (i + 1) * P, :], in_=xres[:, i, :]), r=(f"xres{i}",), dma=f"out{i}")

    for blk in range(nb_pre + nb_own):
        is_pre = blk < nb_pre
        phaseA(blk, is_pre, is_pre and blk == nb_pre - 1)
        if is_pre and blk == nb_pre - 1:
            S.op("dve", lambda e: e.tensor_scalar(out=Cst[:], in0=Cst[:], scalar1=flag[:, 0:1], scalar2=None, op0=ALU.mult), r=("Cst", "flag"), w=("Cst",))
            S.op("dve", lambda e: e.tensor_scalar(out=hist[:], in0=hist[:], scalar1=flag[:, 0:1], scalar2=None, op0=ALU.mult), r=("hist", "flag"), w=("hist",))
        if not is_pre:
            if stop_after != "A":
                phaseP(blk)
            phaseOut(blk)

    finals = [f"out{i}" for i in range(NT)] + (["dbg"] if dbg_out else [])
    S.emit(finals)
    es.close()
    return nc, dbg_out


def prep_weights(inp):
    f = np.float32
    c = lambda a: np.ascontiguousarray(a, dtype=f)
    w = {}
    w["g1T"] = c(inp["norm1_g"].reshape(KD, P).T)
    w["g2T"] = c(inp["norm2_g"].reshape(KD, P).T)
    w["gmixT"] = c(np.concatenate([inp["gm_out_g"], inp["ml_out_g"]]).reshape(KD, P).T)
    w["gf"] = c(inp["final_g"])
    w["gvn"] = c(inp["gm_vnorm_g"])
    w["w_in"] = c(inp["w_in"])
    w["wsT"] = c(np.transpose(inp["w_spatial"], (2, 0, 1)))
    w["bsp"] = c(inp["b_spatial"].T)
    w["cw"] = c(np.transpose(inp["ml_conv_w"].reshape(4, 8, P), (2, 1, 0)))
    w["cb"] = c(inp["ml_conv_b"].reshape(8, P).T)
    w["bi"] = c(inp["ml_b_i"])
    w["bf"] = c(inp["ml_b_f"])
    w["w_out"] = c(inp["w_out"])
    w["wq"] = c(inp["peer_wq"])
    kt = np.zeros((P, 8, 256), f)
    kt[0:64, :, 0:128] = np.transpose(inp["peer_k1"], (2, 0, 1))
    kt[64:128, :, 128:256] = np.transpose(inp["peer_k2"], (2, 0, 1))
    w["kT12"] = kt
    w["UT"] = c(np.transpose(inp["peer_u"].T.reshape(KD, P, NEXP), (1, 0, 2)))
    w["V"] = c(inp["peer_v"])
    w["ident"] = np.eye(P, dtype=f)
    w["maskT"] = np.triu(np.ones((P, P), f))
    return w


def kernel(**inputs):
    x = np.asarray(inputs["x"], dtype=np.float32)
    w = prep_weights(inputs)
    half = SEQ // 2
    nb = half // TB
    nc, _ = build(nb, nb)
    in_maps = []
    for c in range(NCORES):
        b, hf = c // 2, c % 2
        own = np.ascontiguousarray(x[b, hf * half:(hf + 1) * half])
        prev = np.ascontiguousarray(x[b, 0:half]) if hf == 1 else own
        m = dict(w)
        m["xo"] = own
        m["xp"] = prev
        m["flag"] = np.full((P, 1), float(hf), np.float32)
        in_maps.append(m)
    res = run_bass_kernel_spmd(nc, in_maps, core_ids=list(range(NCORES)))
    out = np.empty((BATCH, SEQ, D), np.float32)
    for c in range(NCORES):
        b, hf = c // 2, c % 2
        out[b, hf * half:(hf + 1) * half] = res.results[c]["y"]
    return out
```

```python
import numpy as np
from contextlib import ExitStack
import concourse.bass as bass
import concourse.mybir as mybir
from concourse.bass_utils import run_bass_kernel_spmd

F32 = mybir.dt.float32
BF16 = mybir.dt.bfloat16
AF = mybir.ActivationFunctionType
ALU = mybir.AluOpType
AX = mybir.AxisListType

P = 128
D = 2048
KD = 16
TB = 512
NT = 4
PROJ_W = 5128
EPS = 1e-6
NCORES = 8
SEQ = 8192
BATCH = 4
NEXP = 16384
GC = 4
NG = 128 // GC
GE = GC * 128
NEG = -1.0e30


class Sched:
    ENGS = ("pe", "act", "dve", "pool", "sp")

    def __init__(self, nc):
        self.nc = nc
        self.ops = {e: [] for e in self.ENGS}
        self.dcnt = {}
        self.lastw = {}
        self.readers = {}
        self.waited = {}
        self.same_engine_sync = True
        self.regions = {}
        self.keyreg = {}

    def alias(self, key, region, lo, hi):
        self.regions.setdefault(region, []).append((key, lo, hi))
        self.keyreg[key] = (region, lo, hi)

    def op(self, eng, fn, r=(), w=(), dma=None):
        deps = []
        for k in r:
            t = self.lastw.get(k)
            if t is not None:
                deps.append(t)
        for k in w:
            t = self.lastw.get(k)
            if t is not None:
                deps.append(t)
            deps.extend(self.readers.get(k, ()))
            if k in self.keyreg:
                reg, lo, hi = self.keyreg[k]
                for (k2, lo2, hi2) in self.regions[reg]:
                    if k2 != k and lo < hi2 and lo2 < hi:
                        t = self.lastw.get(k2)
                        if t is not None:
                            deps.append(t)
                        deps.extend(self.readers.get(k2, ()))
        waits = {}
        for (s, v, e) in deps:
            if e == eng and (eng == "pe" or not self.same_engine_sync):
                continue
            if self.waited.get((eng, s), 0) >= v:
                continue
            if waits.get(s, 0) < v:
                waits[s] = v
        for s, v in waits.items():
            self.waited[(eng, s)] = v
        if dma is None:
            s = "E_" + eng
            self.dcnt[s] = self.dcnt.get(s, 0) + 1
            v = self.dcnt[s]
            tok = (s, v, eng)
            self.ops[eng].append([fn, waits, None, v])
        else:
            s = "D_" + dma
            self.dcnt[s] = self.dcnt.get(s, 0) + 16
            v = self.dcnt[s]
            tok = (s, v, "dma")
            self.ops[eng].append([fn, waits, s, v])
        for k in r:
            self.readers.setdefault(k, []).append(tok)
        for k in w:
            self.lastw[k] = tok
            self.readers[k] = []
        return tok

    def emit(self, final_waits):
        nc = self.nc
        needed = {e: set() for e in self.ENGS}
        for e in self.ENGS:
            for (fn, waits, ds, v) in self.ops[e]:
                for s, wv in waits.items():
                    if s.startswith("E_"):
                        needed[s[2:]].add(wv)
        rank = {}
        for e in self.ENGS:
            for i, v in enumerate(sorted(needed[e])):
                rank[(e, v)] = i + 1
        names = set()
        for e in self.ENGS:
            if needed[e]:
                names.add("E_" + e)
            for (fn, waits, ds, v) in self.ops[e]:
                if ds is not None:
                    names.add(ds)
        with ExitStack() as st:
            sems = {n: st.enter_context(nc.semaphore(n)) for n in sorted(names)}
            block = st.enter_context(nc.Block())

            def run(eng_name):
                def body(e):
                    for (fn, waits, ds, v) in self.ops[eng_name]:
                        for s, wv in waits.items():
                            if s.startswith("E_"):
                                e.wait_ge(sems[s], rank[(s[2:], wv)])
                            else:
                                e.wait_ge(sems[s], wv)
                        ins = fn(e)
                        if ds is not None:
                            ins.then_inc(sems[ds], 16)
                        elif v in needed[eng_name]:
                            ins.then_inc(sems["E_" + eng_name], 1)
                    if eng_name == "sp":
                        for s in final_waits:
                            e.wait_ge(sems["D_" + s], self.dcnt["D_" + s])
                return body

            block.tensor(run("pe"))
            block.scalar(run("act"))
            block.vector(run("dve"))
            block.gpsimd(run("pool"))
            block.sync(run("sp"))


def build(nb_pre, nb_own, dbg=(), stop_after=None):
    nc = bass.Bass("TRN2", target_bir_lowering=False)
    ntok = nb_own * TB
    npre = max(nb_pre, 1) * TB

    def din(name, shape):
        return nc.dram_tensor(name, list(shape), F32, kind="ExternalInput").ap()

    xo = din("xo", [ntok, D])
    xp = din("xp", [npre, D])
    flag_d = din("flag", [P, 1])
    g1T_d = din("g1T", [P, KD])
    g2T_d = din("g2T", [P, KD])
    gmixT_d = din("gmixT", [P, KD])
    gf_d = din("gf", [D])
    gvn_d = din("gvn", [1024])
    w_in_d = din("w_in", [D, PROJ_W])
    wsT_d = din("wsT", [P, 8, P])
    bsp_d = din("bsp", [P, 8])
    cw_d = din("cw", [P, 8, 4])
    cb_d = din("cb", [P, 8])
    bi_d = din("bi", [4])
    bf_d = din("bf", [4])
    w_out_d = din("w_out", [D, D])
    wq_d = din("wq", [D, 1024])
    kT12_d = din("kT12", [P, 8, 256])
    UT_d = din("UT", [P, KD, NEXP])
    V_d = din("V", [NEXP, D])
    ident_d = din("ident", [P, P])
    maskT_d = din("maskT", [P, P])
    y_d = nc.dram_tensor("y", [ntok, D], F32, kind="ExternalOutput").ap()
    dbg_out = {}

    w_in_v = w_in_d.rearrange("(k p) n -> p k n", p=P)
    w_out_v = w_out_d.rearrange("(k p) n -> p k n", p=P)
    wq_v = wq_d.rearrange("(k p) n -> p k n", p=P)
    V_v = V_d.rearrange("(c p) d -> p c d", p=P)

    S = Sched(nc)
    es = ExitStack()

    def sb(name, shape, dt):
        return es.enter_context(nc.sbuf_tensor("s_" + name, list(shape), dt))

    def ps(name, shape, dt=F32):
        return es.enter_context(nc.psum_tensor("p_" + name, list(shape), dt))

    xres = sb("xres", [P, NT, D], F32)
    Hreg = sb("Hreg", [P, KD, TB], BF16)
    Wring = [sb(f"W{i}", [P, KD, 512], BF16) for i in range(3)]
    Yreg = sb("Yreg", [P, NT * D], BF16)
    Greg = sb("Greg", [P, 12288], BF16)
    ident = sb("ident", [P, P], BF16)
    maskT = sb("maskT", [P, P], BF16)
    tri32 = sb("tri32", [P, P], F32)
    ones32 = sb("ones32", [P, P], F32)
    g1T = sb("g1T", [P, KD], F32)
    g2T = sb("g2T", [P, KD], F32)
    gmixT = sb("gmixT", [P, KD], F32)
    gvn_rep = sb("gvn_rep", [P, 1024], F32)
    wsT = sb("wsT", [P, 8, P], BF16)
    bsp = sb("bsp", [P, 8], F32)
    cw = sb("cw", [P, 8, 4], F32)
    cb = sb("cb", [P, 8], F32)
    bi_rep = sb("bi_rep", [P, 4], F32)
    bf_rep = sb("bf_rep", [P, 4], F32)
    kT12 = sb("kT12", [P, 8, 256], BF16)
    flag = sb("flag", [P, 1], F32)
    Cst = sb("Cst", [P, 4, 257], F32)
    hist = sb("hist", [P, 8, 3], F32)
    xn = [sb(f"xn{i}", [P, D], BF16) for i in range(2)]
    junk = xn[1]
    ctmp = [sb(f"ctmp{i}", [P, 512], F32) for i in range(2)]
    sg = [sb(f"sg{i}", [P, 512], F32) for i in range(2)]
    qT = sb("qT", [P, 4, TB], BF16)
    kT = sb("kT", [P, 4, TB], BF16)
    ktok = sb("ktok", [P, NT, 4, P], BF16)
    tmpf = sb("tmpf", [P, 1024], F32)
    sTm = sb("sTm", [P, 4, P], BF16)
    Cs_bf = sb("Cs_bf", [P, 4, 257], BF16)
    Cs_f = sb("Cs_f", [P, 4, 257], F32)
    ssq = sb("ssq", [P, 16], F32)
    rstd = sb("rstd", [P, 16], F32)
    if_sb = sb("if_sb", [P, NT, 8], F32)
    gsm = sb("gsm", [P, NT, 32], F32)
    hsm = sb("hsm", [P, 32], F32)

    tv = sb("tv", [P, 32], F32)
    svall = sb("svall", [P, 8, 24], F32)
    psc = sb("psc", [P, NT, 8, 4], F32)
    pst = sb("pst", [P, 8, 24], F32)
    S12 = []
    for i in range(NT):
        base = Yreg[:, i * 4096:(i + 1) * 4096] if i < 2 else Greg[:, (i - 2) * 4096:(i - 1) * 4096]
        S12.append(base.bitcast(F32).rearrange("p (h c) -> p h c", h=8))
    qTp = Greg[:, 8192:12288].rearrange("p (h c) -> p h c", h=8)
    zb = ctmp
    Eb = [sg[j][:].bitcast(BF16)[:, 0:512] for j in range(2)]
    Gh = [sg[j][:].bitcast(BF16)[:, 512:1024] for j in range(2)]
    ga = [xn[0][:, j * 512:(j + 1) * 512] for j in range(4)]
    gax = [sb(f"gax{j}", [P, 512], BF16) for j in range(2)]
    GA = [gax[0][:], gax[1][:]]
    Gacc = [xn[1][:, j * 512:(j + 1) * 512] for j in range(2)]
    GAT = [xn[1][:, 1024 + j * 512:1024 + (j + 1) * 512].rearrange("p (j c) -> p j c", j=4) for j in range(2)]
    ghx = [sb(f"ghx{j}", [P, 512], BF16) for j in range(2)]
    zbx = [sb(f"zbx{j}", [P, 512], F32) for j in range(2)]
    ebx = [sb(f"ebx{j}", [P, 512], BF16) for j in range(2)]
    zb4 = [ctmp[0][:], ctmp[1][:], zbx[0][:], zbx[1][:]]
    zk4 = ["ctmp0", "ctmp1", "zbx0", "zbx1"]
    Eb4 = [Eb[0], Eb[1], ebx[0][:], ebx[1][:]]
    ek4 = ["Eb0", "Eb1", "ebx0", "ebx1"]
    NZ = 4
    Ghr = [Gh[0], Gh[1], ghx[0][:], ghx[1][:]]
    NGH = 4
    for j in range(2):
        S.alias(f"Ghr{j}", "SG%d" % j, 1024, 2048)
    cand = [tmpf[:, j * 256:(j + 1) * 256] for j in range(3)]
    wkb = tmpf[:, 768:896]
    for i in range(NT):
        if i < 2:
            S.alias(f"S12_{i}", "Y", i * 8192, (i + 1) * 8192)
        else:
            S.alias(f"S12_{i}", "G", (i - 2) * 8192, (i - 1) * 8192)
    S.alias("qTp", "G", 16384, 24576)
    S.alias("xn0", "XN0", 0, 4096)
    S.alias("xn1", "XN1", 0, 4096)
    S.alias("junk", "XN1", 0, 4096)
    for j in range(2):
        S.alias(f"ga{j}", "XN0", j * 1024, (j + 1) * 1024)
        S.alias(f"ga{j + 2}", "XN0", (j + 2) * 1024, (j + 3) * 1024)
        S.alias(f"Gacc{j}", "XN1", j * 1024, (j + 1) * 1024)
        S.alias(f"GAT{j}", "XN1", 2048 + j * 1024, 2048 + (j + 1) * 1024)
        S.alias(f"sg{j}", "SG%d" % j, 0, 2048)
        S.alias(f"Eb{j}", "SG%d" % j, 0, 1024)
        S.alias(f"Gh{j}", "SG%d" % j, 1024, 2048)

    G32 = Greg[:, 0:8192].bitcast(F32).rearrange("p (i c) -> p i c", i=NT)
    vg = Greg[:, 8192:12288].rearrange("p (i c) -> p i c", i=NT)
    vaug = Greg[:, 0:4112].rearrange("p (i h c) -> p i h c", i=NT, h=4)
    so = Greg[:, 4224:8320].rearrange("p (i c) -> p i c", i=NT)
    pre = [Greg[:, 8448 + j * 1040: 8448 + j * 1040 + 1030].bitcast(F32) for j in range(2)]
    ymix = Yreg[:].rearrange("p (i c) -> p i c", i=NT)

    for i in range(NT):
        S.alias(f"G32_{i}", "G", i * 4096, (i + 1) * 4096)
        S.alias(f"vg{i}", "G", 16384 + i * 2048, 16384 + (i + 1) * 2048)
        S.alias(f"vaug{i}", "G", i * 2056, (i + 1) * 2056)
        S.alias(f"so{i}", "G", 8448 + i * 2048, 8448 + (i + 1) * 2048)
        S.alias(f"ymix{i}", "Y", i * 4096, (i + 1) * 4096)
    for j in range(2):
        S.alias(f"pre{j}", "G", 16896 + j * 2080, 16896 + j * 2080 + 2060)

    PTt = ps("PT", [P, 1024])
    PPt = ps("PP", [P, 1024])
    PBt = ps("PB", [P, 2048])
    PT = PTt[:].bitcast(BF16).rearrange("p (k c) -> p k c", k=KD)
    PP = [PPt[:, 0:512], PPt[:, 512:1024]]
    PM = PBt[:, 0:1024]
    PS = PBt[:, 1024:1536]
    PX = PBt[:, 1536:2048]
    PXbf = PBt[:, 1536 + 128:1536 + 384].bitcast(BF16).rearrange("p (h c) -> p h c", h=4)

    cnt = {"pp": 0, "w": 0, "xn": 0, "pre": 0, "ct": 0, "ga": 0}

    def nxt(name, n):
        v = cnt[name] % n
        cnt[name] += 1
        return v

    def ld(eng, out, in_, key, slot):
        S.op(eng, lambda e: e.dma_start(out=out, in_=in_), r=(), w=(key,), dma="c_" + key)

    ld("pool", ident[:], ident_d, "ident", "c0")
    ld("pool", maskT[:], maskT_d, "maskT", "c0")
    ld("pool", wsT[:], wsT_d, "wsT", "c0")
    ld("pool", kT12[:], kT12_d, "kT12", "c0")
    ld("sp", tri32[:], maskT_d, "tri32", "c1")
    ld("sp", g1T[:], g1T_d, "g1T", "c1")
    ld("sp", g2T[:], g2T_d, "g2T", "c1")
    ld("sp", gmixT[:], gmixT_d, "gmixT", "c1")
    ld("sp", gvn_rep[:], gvn_d.partition_broadcast(P), "gvn_rep", "c1")
    ld("sp", bsp[:], bsp_d, "bsp", "c1")
    ld("sp", cw[:], cw_d, "cw", "c1")
    ld("sp", cb[:], cb_d, "cb", "c1")
    ld("sp", bi_rep[:], bi_d.partition_broadcast(P), "bi_rep", "c1")
    ld("sp", bf_rep[:], bf_d.partition_broadcast(P), "bf_rep", "c1")
    ld("sp", flag[:], flag_d, "flag", "c1")
    S.op("dve", lambda e: e.memset(ones32[:], 1.0), w=("ones32",))
    S.op("dve", lambda e: e.memset(Cst[:], 0.0), w=("Cst",))
    S.op("dve", lambda e: e.memset(hist[:], 0.0), w=("hist",))
    S.op("dve", lambda e: e.tensor_tensor(out=wsT[:], in0=wsT[:], in1=maskT[:].unsqueeze(1).to_broadcast([P, 8, P]), op=ALU.mult),
         r=("wsT", "maskT"), w=("wsT",))

    def wload(src_ap, ncols=512):
        slot = nxt("w", 3)
        wt = Wring[slot]
        dst = wt[:, :, 0:ncols]
        S.op("pool", lambda e: e.dma_start(out=dst, in_=src_ap), w=(f"W{slot}",), dma=f"W{slot}")
        return wt, f"W{slot}"

    def rms_rstd(src_ap, key_src, col, n):
        S.op("act", lambda e: e.activation(out=junk[:, 0:n], in_=src_ap, func=AF.Square, accum_out=ssq[:, col:col + 1]),
             r=(key_src,), w=("junk", f"ssq{col}"))
        S.op("act", lambda e: e.activation(out=ssq[:, col:col + 1], in_=ssq[:, col:col + 1], func=AF.Sqrt, scale=1.0 / n, bias=eps_t[:, 0:1]),
             r=(f"ssq{col}", "eps"), w=(f"ssq{col}",))
        S.op("dve", lambda e: e.reciprocal(out=rstd[:, col:col + 1], in_=ssq[:, col:col + 1]), r=(f"ssq{col}",), w=(f"rstd{col}",))

    eps_t = sb("eps_t", [P, 1], F32)
    S.op("dve", lambda e: e.memset(eps_t[:], EPS), w=("eps",))

    def norm_transpose(i, src_ap, key_src, gT, gkey, dst_keyname):
        col = i
        rms_rstd(src_ap, key_src, col, D)
        xs = nxt("xn", 2)
        xnb = xn[xs]
        S.op("dve", lambda e: e.tensor_scalar(out=xnb[:], in0=src_ap, scalar1=rstd[:, col:col + 1], scalar2=None, op0=ALU.mult),
             r=(key_src, f"rstd{col}"), w=(f"xn{xs}",))
        for k in range(KD):
            S.op("pe", lambda e, k=k: e.transpose(out=PT[:, k, :], in_=xnb[:, k * P:(k + 1) * P], identity=ident[:]),
                 r=(f"xn{xs}", "ident"), w=("PT0", "PT1"))
        S.op("dve", lambda e: e.tensor_tensor(out=Hreg[:, :, i * P:(i + 1) * P], in0=PT, in1=gT[:].unsqueeze(2).to_broadcast([P, KD, P]), op=ALU.mult),
             r=("PT0", "PT1", gkey), w=(f"{dst_keyname}{i}",))

    def proj_tok(i, hkey, wt, wkey, ncols):
        s = nxt("pp", 2)
        out = PP[s][:, 0:ncols]
        for k in range(KD):
            S.op("pe", lambda e, k=k: e.matmul(out, lhsT=Hreg[:, k, i * P:(i + 1) * P], rhs=wt[:, k, 0:ncols], start=(k == 0), stop=(k == KD - 1)),
                 r=(f"{hkey}{i}", wkey), w=(f"PP{s}",))
        return PP[s], f"PP{s}"

    def proj_feat(hkey, wt, wkey, c0):
        s = nxt("pp", 2)
        out = PP[s]
        for k in range(KD):
            S.op("pe", lambda e, k=k: e.matmul(out, lhsT=wt[:, k, c0:c0 + P], rhs=Hreg[:, k, :], start=(k == 0), stop=(k == KD - 1)),
                 r=tuple(f"{hkey}{i}" for i in range(NT)) + (wkey,), w=(f"PP{s}",))
        return PP[s], f"PP{s}"

    def dbg_dump(name, ap, shape, key):
        if name not in dbg:
            return
        t = nc.dram_tensor("dbg_" + name, list(shape), F32, kind="ExternalOutput").ap()
        dbg_out[name] = t
        S.op("sp", lambda e: e.dma_start(out=t, in_=ap), r=key, dma="dbg")

    def phaseA(blk, is_pre, last_pre):
        xsrc = xp if is_pre else xo
        b0 = (blk if is_pre else blk - nb_pre) * TB
        for i in range(NT):
            S.op("sp", lambda e, i=i: e.dma_start(out=xres[:, i, :], in_=xsrc[b0 + i * P: b0 + (i + 1) * P, :]),
                 w=(f"xres{i}",), dma=f"x{i}")
            norm_transpose(i, xres[:, i, :], f"xres{i}", g1T, "g1T", "hT")
        if "hT" in dbg and not is_pre and blk == nb_pre:
            pass

        wt, wk_ = wload(w_in_v[:, :, 5120:5128], 8)
        for i in range(NT):
            pp, pk = proj_tok(i, "hT", wt, wk_, 8)
            S.op("act", lambda e, i=i, pp=pp: e.copy(out=if_sb[:, i, :], in_=pp[:, 0:8]), r=(pk,), w=(f"if{i}",))
            g = gsm[:, i, :]
            S.op("dve", lambda e, i=i, g=g: e.tensor_tensor(out=g[:, 20:24], in0=if_sb[:, i, 4:8], in1=bf_rep[:], op=ALU.add),
                 r=(f"if{i}", "bf_rep"), w=(f"g{i}",))
            S.op("act", lambda e, g=g: e.activation(out=g[:, 20:24], in_=g[:, 20:24], func=AF.Exp, scale=-1.0), r=(f"g{i}",), w=(f"g{i}",))
            S.op("act", lambda e, g=g: e.activation(out=g[:, 0:4], in_=g[:, 20:24], func=AF.Ln, bias=one_t[:, 0:1]), r=(f"g{i}", "one"), w=(f"g{i}",))
            S.op("pe", lambda e, g=g: e.matmul(PX[:, 0:4], lhsT=tri32[:], rhs=g[:, 0:4], start=True, stop=True), r=(f"g{i}", "tri32"), w=("PX",))
            S.op("pe", lambda e, g=g: e.matmul(PX[:, 4:8], lhsT=ones32[:], rhs=g[:, 0:4], start=True, stop=True), r=(f"g{i}", "ones32"), w=("PX",))
            S.op("act", lambda e, g=g: e.copy(out=g[:, 24:32], in_=PX[:, 0:8]), r=("PX",), w=(f"g{i}",))
            S.op("dve", lambda e, g=g: e.tensor_tensor(out=g[:, 4:8], in0=g[:, 24:28], in1=g[:, 28:32], op=ALU.subtract), r=(f"g{i}",), w=(f"g{i}",))
            S.op("dve", lambda e, i=i, g=g: e.tensor_tensor(out=g[:, 20:24], in0=if_sb[:, i, 0:4], in1=bi_rep[:], op=ALU.add), r=(f"if{i}", "bi_rep", f"g{i}"), w=(f"g{i}",))
            S.op("dve", lambda e, g=g: e.tensor_tensor(out=g[:, 20:24], in0=g[:, 20:24], in1=g[:, 4:8], op=ALU.add), r=(f"g{i}",), w=(f"g{i}",))
            S.op("act", lambda e, g=g: e.activation(out=g[:, 8:12], in_=g[:, 20:24], func=AF.Exp), r=(f"g{i}",), w=(f"g{i}",))
            S.op("act", lambda e, g=g: e.activation(out=g[:, 12:16], in_=g[:, 4:8], func=AF.Exp), r=(f"g{i}",), w=(f"g{i}",))
            S.op("act", lambda e, g=g: e.activation(out=g[:, 16:20], in_=g[:, 28:32], func=AF.Exp, scale=-1.0), r=(f"g{i}",), w=(f"g{i}",))

        if not is_pre:
            for half in range(2):
                wt, wk_ = wload(w_in_v[:, :, 1024 + half * 512: 1024 + (half + 1) * 512])
                for i in range(NT):
                    pp, pk = proj_tok(i, "hT", wt, wk_, 512)
                    S.op("act", lambda e, i=i, pp=pp, half=half: e.activation(out=G32[:, i, half * 512:(half + 1) * 512], in_=pp, func=AF.Gelu_apprx_tanh),
                         r=(pk,), w=(f"G32_{i}",))
            for i in range(NT):
                rms_rstd(G32[:, i, :], f"G32_{i}", 4 + i, 1024)
                S.op("dve", lambda e, i=i: e.scalar_tensor_tensor(out=vg[:, i, :], in0=G32[:, i, :], scalar=rstd[:, 4 + i:5 + i], in1=gvn_rep[:], op0=ALU.mult, op1=ALU.mult),
                     r=(f"G32_{i}", f"rstd{4 + i}", "gvn_rep"), w=(f"vg{i}",))
            for i in range(NT):
                for h in range(8):
                    S.op("pe", lambda e, i=i, h=h: e.matmul(PM[:, h * P:(h + 1) * P], lhsT=wsT[:, h, :], rhs=vg[:, i, h * P:(h + 1) * P], start=True, stop=True),
                         r=("wsT", f"vg{i}"), w=("PM",))
                S.op("dve", lambda e, i=i: e.tensor_tensor(out=G32[:, i, :].rearrange("p (h c) -> p h c", h=8), in0=PM.rearrange("p (h c) -> p h c", h=8),
                                                           in1=bsp[:].unsqueeze(2).to_broadcast([P, 8, P]), op=ALU.add),
                     r=("PM", "bsp", f"vg{i}"), w=(f"G32_{i}",))
            for half in range(2):
                wt, wk_ = wload(w_in_v[:, :, half * 512:(half + 1) * 512])
                for i in range(NT):
                    pp, pk = proj_tok(i, "hT", wt, wk_, 512)
                    c = nxt("ct", 2)
                    S.op("act", lambda e, pp=pp, c=c: e.activation(out=ctmp[c][:], in_=pp, func=AF.Gelu_apprx_tanh), r=(pk,), w=(f"ctmp{c}",))
                    S.op("dve", lambda e, i=i, half=half, c=c: e.tensor_tensor(out=G32[:, i, half * 512:(half + 1) * 512], in0=G32[:, i, half * 512:(half + 1) * 512], in1=ctmp[c][:], op=ALU.mult),
                         r=(f"ctmp{c}", f"G32_{i}"), w=(f"G32_{i}",))
            for i in range(NT):
                for h in range(8):
                    S.op("act", lambda e, i=i, h=h: e.activation(out=junk[:, 0:P], in_=G32[:, i, h * P:(h + 1) * P], func=AF.Square, accum_out=hsm[:, h:h + 1]),
                         r=(f"G32_{i}",), w=("junk", "hsm"))
                S.op("act", lambda e: e.activation(out=hsm[:, 0:8], in_=hsm[:, 0:8], func=AF.Sqrt, scale=1.0 / P, bias=eps_t[:, 0:1]), r=("hsm", "eps"), w=("hsm",))
                S.op("dve", lambda e: e.reciprocal(out=hsm[:, 8:16], in_=hsm[:, 0:8]), r=("hsm",), w=("hsm",))
                S.op("dve", lambda e, i=i: e.tensor_tensor(out=ymix[:, i, 0:1024].rearrange("p (h c) -> p h c", h=8), in0=G32[:, i, :].rearrange("p (h c) -> p h c", h=8),
                                                           in1=hsm[:, 8:16].unsqueeze(2).to_broadcast([P, 8, P]), op=ALU.mult),
                     r=(f"G32_{i}", "hsm"), w=(f"ymix{i}",))

        chunks = list(range(8)) if (not is_pre or last_pre) else list(range(4, 8))
        for part in ((0, 1) if (not is_pre or last_pre) else (1,)):
            wt, wk_ = wload(w_in_v[:, :, 2048 + part * 512: 2048 + (part + 1) * 512])
            for jj in range(4):
                j = part * 4 + jj
                pp, pk = proj_feat("hT", wt, wk_, jj * P)
                pr = nxt("pre", 2)
                pb = pre[pr]
                S.op("act", lambda e, pb=pb, pp=pp: e.copy(out=pb[:, 3:515], in_=pp), r=(pk,), w=(f"pre{pr}",))
                S.op("dve", lambda e, pb=pb, j=j: e.tensor_copy(out=pb[:, 0:3], in_=hist[:, j, :]), r=("hist",), w=(f"pre{pr}",))
                S.op("dve", lambda e, pb=pb, j=j: e.tensor_copy(out=hist[:, j, :], in_=pb[:, 512:515]), r=(f"pre{pr}",), w=("hist",))
                if is_pre and part == 0:
                    continue
                c = nxt("ct", 2)
                ct = ctmp[c]
                S.op("dve", lambda e, pb=pb, j=j, ct=ct: e.tensor_scalar(out=ct[:], in0=pb[:, 0:512], scalar1=cw[:, j, 0:1], scalar2=cb[:, j:j + 1], op0=ALU.mult, op1=ALU.add),
                     r=(f"pre{pr}", "cw", "cb"), w=(f"ctmp{c}",))
                for tap in range(1, 4):
                    S.op("dve", lambda e, pb=pb, j=j, ct=ct, tap=tap: e.scalar_tensor_tensor(out=ct[:], in0=pb[:, tap:tap + 512], scalar=cw[:, j, tap:tap + 1], in1=ct[:], op0=ALU.mult, op1=ALU.add),
                         r=(f"pre{pr}", "cw", f"ctmp{c}"), w=(f"ctmp{c}",))
                S.op("act", lambda e, ct=ct, c=c: e.activation(out=sg[c][:], in_=ct[:], func=AF.Sigmoid), r=(f"ctmp{c}",), w=(f"sg{c}",))
                if part == 0:
                    S.op("dve", lambda e, ct=ct, c=c, jj=jj: e.tensor_tensor(out=qT[:, jj, :], in0=ct[:], in1=sg[c][:], op=ALU.mult),
                         r=(f"ctmp{c}", f"sg{c}"), w=("qT",))
                else:
                    S.op("dve", lambda e, ct=ct, c=c, jj=jj: e.scalar_tensor_tensor(out=kT[:, jj, :], in0=ct[:], scalar=float(P ** -0.5), in1=sg[c][:], op0=ALU.mult, op1=ALU.mult),
                         r=(f"ctmp{c}", f"sg{c}"), w=("kT",))

        for half in range(2):
            wt, wk_ = wload(w_in_v[:, :, 3072 + half * 512: 3072 + (half + 1) * 512])
            for i in range(NT):
                pp, pk = proj_tok(i, "hT", wt, wk_, 512)
                S.op("dve", lambda e, i=i, half=half, pp=pp: e.tensor_tensor(out=vaug[:, i, 2 * half:2 * half + 2, 0:256], in0=pp.rearrange("p (h c) -> p h c", h=2),
                                                                          in1=gsm[:, i, 8 + 2 * half:10 + 2 * half].unsqueeze(2).to_broadcast([P, 2, 256]), op=ALU.mult),
                     r=(pk, f"g{i}"), w=(f"vaug{i}",))
        for i in range(NT):
            S.op("dve", lambda e, i=i: e.tensor_copy(out=vaug[:, i, :, 256:257], in_=gsm[:, i, 8:12].unsqueeze(2)), r=(f"g{i}",), w=(f"vaug{i}",))
        if not is_pre:
            for half in range(2):
                wt, wk_ = wload(w_in_v[:, :, 4096 + half * 512: 4096 + (half + 1) * 512])
                for i in range(NT):
                    pp, pk = proj_tok(i, "hT", wt, wk_, 512)
                    S.op("act", lambda e, i=i, half=half, pp=pp: e.activation(out=so[:, i, half * 512:(half + 1) * 512], in_=pp, func=AF.Sigmoid), r=(pk,), w=(f"so{i}",))

        for i in range(NT):
            tsl = slice(i * P, (i + 1) * P)
            g = gsm[:, i, :]
            for h in range(4):
                S.op("pe", lambda e, h=h, tsl=tsl: e.transpose(out=PXbf[:, h, :], in_=kT[:, h, tsl], identity=ident[:]), r=("kT", "ident"), w=("PX",))
            S.op("act", lambda e, i=i: e.copy(out=ktok[:, i, :, :], in_=PXbf), r=("PX",), w=(f"ktok{i}",))
            S.op("dve", lambda e, g=g: e.tensor_tensor(out=Cs_f[:], in0=Cst[:], in1=g[:, 16:20].unsqueeze(2).to_broadcast([P, 4, 257]), op=ALU.mult),
                 r=("Cst", f"g{i}"), w=("Cs_f",))
            if not is_pre:
                S.op("act", lambda e: e.copy(out=Cs_bf[:], in_=Cs_f[:]), r=("Cs_f",), w=("Cs_bf",))
                for h in range(4):
                    S.op("pe", lambda e, h=h, tsl=tsl: e.matmul(PS[:, h * P:(h + 1) * P], lhsT=kT[:, h, tsl], rhs=qT[:, h, tsl], start=True, stop=True),
                         r=("kT", "qT"), w=("PS",))
                S.op("dve", lambda e: e.tensor_tensor(out=sTm[:], in0=PS.rearrange("p (h c) -> p h c", h=4), in1=maskT[:].unsqueeze(1).to_broadcast([P, 4, P]), op=ALU.mult),
                     r=("PS", "maskT"), w=("sTm",))
                for h in range(4):
                    S.op("pe", lambda e, h=h, i=i: e.matmul(PM[:, h * 256:(h + 1) * 256], lhsT=sTm[:, h, :], rhs=vaug[:, i, h, 0:256], start=True, stop=False),
                         r=("sTm", f"vaug{i}"), w=("PM",))
                    S.op("pe", lambda e, h=h, tsl=tsl: e.matmul(PM[:, h * 256:(h + 1) * 256], lhsT=qT[:, h, tsl], rhs=Cs_bf[:, h, 0:256], start=False, stop=True),
                         r=("qT", "Cs_bf"), w=("PM",))
                    S.op("pe", lambda e, h=h, i=i: e.matmul(PX[:, 8 + h:9 + h], lhsT=sTm[:, h, :], rhs=vaug[:, i, h, 256:257], start=True, stop=False),
                         r=("sTm", f"vaug{i}"), w=("PX",))
                    S.op("pe", lambda e, h=h, tsl=tsl: e.matmul(PX[:, 8 + h:9 + h], lhsT=qT[:, h, tsl], rhs=Cs_bf[:, h, 256:257], start=False, stop=True),
                         r=("qT", "Cs_bf"), w=("PX",))
            PPC = PPt[:].rearrange("p (h c) -> p h c", h=4)
            for h in range(4):
                S.op("pe", lambda e, h=h, i=i: e.matmul(PPt[:, h * 256:(h + 1) * 256], lhsT=ktok[:, i, h, :], rhs=vaug[:, i, h, 0:256], start=True, stop=True),
                     r=(f"ktok{i}", f"vaug{i}"), w=("PP0", "PP1"))
                S.op("pe", lambda e, h=h, i=i: e.matmul(PX[:, 12 + h:13 + h], lhsT=ktok[:, i, h, :], rhs=vaug[:, i, h, 256:257], start=True, stop=True),
                     r=(f"ktok{i}", f"vaug{i}"), w=("PX",))
            S.op("dve", lambda e: e.tensor_tensor(out=Cst[:, :, 0:256], in0=Cs_f[:, :, 0:256], in1=PPC, op=ALU.add), r=("Cs_f", "PP0", "PP1"), w=("Cst",))
            S.op("dve", lambda e: e.tensor_tensor(out=Cst[:, :, 256:257], in0=Cs_f[:, :, 256:257], in1=PX[:, 12:16].unsqueeze(2), op=ALU.add), r=("Cs_f", "PX"), w=("Cst",))
            if is_pre:
                continue
            S.op("act", lambda e: e.activation(out=hsm[:, 16:20], in_=PX[:, 8:12], func=AF.Abs), r=("PX",), w=("hsm2",))
            S.op("dve", lambda e, g=g: e.tensor_tensor(out=hsm[:, 16:20], in0=hsm[:, 16:20], in1=g[:, 12:16], op=ALU.max), r=("hsm2", f"g{i}"), w=("hsm2",))
            S.op("dve", lambda e: e.reciprocal(out=hsm[:, 20:24], in_=hsm[:, 16:20]), r=("hsm2",), w=("hsm2",))
            S.op("dve", lambda e: e.tensor_tensor(out=tmpf[:].rearrange("p (h c) -> p h c", h=4), in0=PM.rearrange("p (h c) -> p h c", h=4),
                                                  in1=hsm[:, 20:24].unsqueeze(2).to_broadcast([P, 4, 256]), op=ALU.mult), r=("PM", "hsm2"), w=("tmpf",))
            S.op("dve", lambda e, i=i: e.tensor_tensor(out=tmpf[:], in0=tmpf[:], in1=so[:, i, :], op=ALU.mult), r=("tmpf", f"so{i}"), w=("tmpf",))
            for h in range(4):
                S.op("act", lambda e, h=h: e.activation(out=junk[:, 0:256], in_=tmpf[:, h * 256:(h + 1) * 256], func=AF.Square, accum_out=hsm[:, 24 + h:25 + h]),
                     r=("tmpf",), w=("junk", "hsm3"))
            S.op("act", lambda e: e.activation(out=hsm[:, 24:28], in_=hsm[:, 24:28], func=AF.Sqrt, scale=1.0 / 256, bias=eps_t[:, 0:1]), r=("hsm3", "eps"), w=("hsm3",))
            S.op("dve", lambda e: e.reciprocal(out=hsm[:, 28:32], in_=hsm[:, 24:28]), r=("hsm3",), w=("hsm3",))
            S.op("dve", lambda e, i=i: e.tensor_tensor(out=ymix[:, i, 1024:2048].rearrange("p (h c) -> p h c", h=4), in0=tmpf[:].rearrange("p (h c) -> p h c", h=4),
                                                       in1=hsm[:, 28:32].unsqueeze(2).to_broadcast([P, 4, 256]), op=ALU.mult), r=("tmpf", "hsm3"), w=(f"ymix{i}",))
        if is_pre:
            return
        for i in range(NT):
            for k in range(KD):
                S.op("pe", lambda e, k=k, i=i: e.transpose(out=PT[:, k, :], in_=ymix[:, i, k * P:(k + 1) * P], identity=ident[:]),
                     r=(f"ymix{i}", "ident"), w=("PT0", "PT1"))
            S.op("dve", lambda e, i=i: e.tensor_tensor(out=Hreg[:, :, i * P:(i + 1) * P], in0=PT, in1=gmixT[:].unsqueeze(2).to_broadcast([P, KD, P]), op=ALU.mult),
                 r=("PT0", "PT1", "gmixT"), w=(f"hT{i}",))
        for gcol in range(4):
            wt, wk_ = wload(w_out_v[:, :, gcol * 512:(gcol + 1) * 512])
            for i in range(NT):
                pp, pk = proj_tok(i, "hT", wt, wk_, 512)
                S.op("dve", lambda e, i=i, gcol=gcol, pp=pp: e.tensor_tensor(out=xres[:, i, gcol * 512:(gcol + 1) * 512], in0=xres[:, i, gcol * 512:(gcol + 1) * 512], in1=pp, op=ALU.add),
                     r=(pk, f"xres{i}"), w=(f"xres{i}",))

    one_t = sb("one_t", [P, 1], F32)
    S.op("dve", lambda e: e.memset(one_t[:], 1.0), w=("one",))

    def phaseP(blk):
        for i in range(NT):
            norm_transpose(i, xres[:, i, :], f"xres{i}", g2T, "g2T", "hT")
        for part in range(2):
            wt, wk_ = wload(wq_v[:, :, part * 512:(part + 1) * 512])
            for hh in range(4):
                h = part * 4 + hh
                pp, pk = proj_feat("hT", wt, wk_, hh * P)
                S.op("act", lambda e, h=h, pp=pp: e.copy(out=qTp[:, h, :], in_=pp), r=(pk,), w=("qTp",))
        for i in range(NT):
            tsl = slice(i * P, (i + 1) * P)
            for hp in range(2):
                for hh in range(4):
                    h = hp * 4 + hh
                    S.op("pe", lambda e, h=h, hh=hh, tsl=tsl: e.matmul(PM[:, hh * 256:(hh + 1) * 256], lhsT=qTp[:, h, tsl], rhs=kT12[:, h, :], start=True, stop=True),
                         r=("qTp", "kT12"), w=("PM",))
                S.op("act", lambda e, i=i, hp=hp: e.copy(out=S12[i][:, hp * 4:(hp + 1) * 4, :], in_=PM.rearrange("p (h c) -> p h c", h=4)),
                     r=("PM",), w=(f"S12_{i}",))
        for i in range(NT):
            for h in range(8):
                for half in range(2):
                    src = S12[i][:, h, half * P:(half + 1) * P]
                    o = half * 16
                    S.op("dve", lambda e, src=src, o=o: e.max(out=tv[:, o:o + 8], in_=src), r=(f"S12_{i}",), w=("tv",))
                    S.op("dve", lambda e, src=src, o=o: e.match_replace(out=wkb, in_to_replace=tv[:, o:o + 8], in_values=src, imm_value=NEG),
                         r=(f"S12_{i}", "tv"), w=("tmpf",))
                    S.op("dve", lambda e, o=o: e.max(out=tv[:, o + 8:o + 16], in_=wkb), r=("tmpf",), w=("tv",))
                S.op("dve", lambda e: e.tensor_tensor(out=cand[0].rearrange("p (a b) -> p a b", a=16), in0=tv[:, 0:16].unsqueeze(2).to_broadcast([P, 16, 16]),
                                                      in1=tv[:, 16:32].unsqueeze(1).to_broadcast([P, 16, 16]), op=ALU.add), r=("tv",), w=("tmpf",))
                S.op("dve", lambda e, h=h: e.max(out=svall[:, h, 0:8], in_=cand[0]), r=("tmpf",), w=("svall",))
                S.op("dve", lambda e, h=h: e.match_replace(out=cand[1], in_to_replace=svall[:, h, 0:8], in_values=cand[0], imm_value=NEG), r=("tmpf", "svall"), w=("tmpf",))
                S.op("dve", lambda e, h=h: e.max(out=svall[:, h, 8:16], in_=cand[1]), r=("tmpf",), w=("svall",))
                S.op("dve", lambda e, h=h: e.match_replace(out=cand[2], in_to_replace=svall[:, h, 8:16], in_values=cand[1], imm_value=NEG), r=("tmpf", "svall"), w=("tmpf",))
                S.op("dve", lambda e, h=h: e.max(out=svall[:, h, 16:24], in_=cand[2]), r=("tmpf",), w=("svall",))
            S.op("dve", lambda e: e.tensor_tensor(out=pst[:, :, 0:16], in0=svall[:, :, 0:16], in1=svall[:, :, 0:1].to_broadcast([P, 8, 16]), op=ALU.subtract),
                 r=("svall",), w=("pst",))
            S.op("act", lambda e: e.activation(out=pst[:, :, 0:16], in_=pst[:, :, 0:16], func=AF.Exp), r=("pst",), w=("pst",))
            S.op("dve", lambda e: e.reduce_sum(out=pst[:, :, 16:17], in_=pst[:, :, 0:16], axis=AX.X), r=("pst",), w=("pst",))
            S.op("act", lambda e: e.activation(out=pst[:, :, 17:18], in_=pst[:, :, 16:17], func=AF.Ln), r=("pst",), w=("pst",))
            S.op("dve", lambda e, i=i: e.scalar_tensor_tensor(out=psc[:, i, :, 0:1], in0=svall[:, :, 0:1], scalar=-1.0, in1=pst[:, :, 17:18], op0=ALU.mult, op1=ALU.subtract),
                 r=("svall", "pst"), w=(f"psc{i}",))
            S.op("dve", lambda e: e.tensor_tensor(out=pst[:, :, 18:19], in0=svall[:, :, 15:16], in1=svall[:, :, 16:17], op=ALU.add), r=("svall", "pst"), w=("pst",))
            S.op("dve", lambda e, i=i: e.scalar_tensor_tensor(out=psc[:, i, :, 1:2], in0=pst[:, :, 18:19], scalar=0.5, in1=psc[:, i, :, 0:1], op0=ALU.mult, op1=ALU.add),
                 r=("pst", f"psc{i}"), w=(f"psc{i}",))
            S.op("dve", lambda e, i=i: e.tensor_scalar(out=psc[:, i, :, 2:3], in0=psc[:, i, :, 1:2], scalar1=-1.0, scalar2=None, op0=ALU.mult),
                 r=(f"psc{i}",), w=(f"psc{i}",))
            S.op("dve", lambda e, i=i: e.tensor_tensor(out=psc[:, i, :, 3:4], in0=psc[:, i, :, 0:1], in1=psc[:, i, :, 2:3], op=ALU.add),
                 r=(f"psc{i}",), w=(f"psc{i}",))
            S.op("dve", lambda e, i=i: e.tensor_tensor(out=S12[i][:, :, P:2 * P], in0=S12[i][:, :, P:2 * P], in1=psc[:, i, :, 3:4].to_broadcast([P, 8, P]), op=ALU.add),
                 r=(f"S12_{i}", f"psc{i}"), w=(f"S12_{i}",))
        units = [(g, i) for g in range(NG) for i in range(NT)]
        NU = len(units)
        wts = {}

        def load_u(g):
            wts[("u", g)] = wload(UT_d[:, :, g * GE:(g + 1) * GE])

        def load_v(g):
            slot = nxt("w", 3)
            vt = Wring[slot][:].rearrange("p k c -> p (k c)").rearrange("p (j d) -> p j d", j=GC)
            vk = f"W{slot}"
            S.op("pool", lambda e, vt=vt, g=g: e.dma_start(out=vt, in_=V_v[:, g * GC:(g + 1) * GC, :]), w=(vk,), dma=vk)
            wts[("v", g)] = (vt, vk)

        st = {}

        def stageA_pe(u):
            g, i = units[u]
            ut, uk = wts[("u", g)]
            sl = u % 2
            for k in range(KD):
                S.op("pe", lambda e, k=k, i=i, ut=ut, sl=sl: e.matmul(PP[sl], lhsT=Hreg[:, k, i * P:(i + 1) * P], rhs=ut[:, k, 0:GE], start=(k == 0), stop=(k == KD - 1)),
                     r=(f"hT{i}", uk), w=(f"PP{sl}",))

        def gelu_pair(kp):
            s0 = (2 * kp) % 4
            S.op("act", lambda e, s0=s0: e.activation(out=xn[0][:, s0 * 512:(s0 + 2) * 512], in_=PPt[:, 0:1024], func=AF.Gelu_apprx_tanh),
                 r=("PP0", "PP1"), w=(f"ga{s0}", f"ga{s0 + 1}"))

        DVEH = (1, 2)
        BIGA = 1.0e7

        def Z(n):
            u, h = divmod(n, 8)
            g, i = units[u]
            a = u % 2
            c = n % NZ
            S.op("dve", lambda e, i=i, h=h, c=c, g=g: e.tensor_tensor(
                out=zb4[c].rearrange("p (j c) -> p j c", j=GC),
                in0=S12[i][:, h, P:2 * P].unsqueeze(1).to_broadcast([P, GC, P]),
                in1=S12[i][:, h, g * GC:(g + 1) * GC].unsqueeze(2).to_broadcast([P, GC, P]),
                op=ALU.add), r=(f"S12_{i}",), w=(zk4[c],))
            if h not in DVEH:
                S.op("act", lambda e, i=i, h=h, c=c: e.activation(out=zb4[c], in_=zb4[c], func=AF.Prelu, alpha=BIGA),
                     r=(zk4[c],), w=(zk4[c],))
                if h == 0:
                    S.op("act", lambda e, i=i, h=h, c=c, a=a: e.activation(out=Gacc[a], in_=zb4[c], func=AF.Exp, bias=psc[:, i, h, 1:2]),
                         r=(zk4[c], f"psc{i}"), w=(f"Gacc{a}",))
                else:
                    S.op("act", lambda e, i=i, h=h, c=c: e.activation(out=Eb4[c], in_=zb4[c], func=AF.Exp, bias=psc[:, i, h, 1:2]),
                         r=(zk4[c], f"psc{i}"), w=(ek4[c],))
            else:
                S.op("act", lambda e, i=i, h=h, c=c: e.activation(out=Eb4[c], in_=zb4[c], func=AF.Exp, bias=psc[:, i, h, 1:2]), r=(zk4[c], f"psc{i}"), w=(ek4[c],))

        def M(n):
            u, h = divmod(n, 8)
            g, i = units[u]
            a = u % 2
            c = n % NZ
            if h == 0:
                return
            if h not in DVEH:
                S.op("dve", lambda e, c=c, a=a: e.tensor_tensor(out=Gacc[a], in0=Gacc[a], in1=Eb4[c], op=ALU.add), r=(f"Gacc{a}", ek4[c]), w=(f"Gacc{a}",))
            else:
                q = h % NGH
                S.op("dve", lambda e, i=i, h=h, c=c, q=q: e.scalar_tensor_tensor(out=Ghr[q], in0=zb4[c], scalar=0.0, in1=Eb4[c], op0=ALU.is_ge, op1=ALU.mult),
                     r=(zk4[c], ek4[c]), w=(f"Ghr{q}",))
                S.op("dve", lambda e, q=q, a=a: e.tensor_tensor(out=Gacc[a], in0=Gacc[a], in1=Ghr[q], op=ALU.add), r=(f"Gacc{a}", f"Ghr{q}"), w=(f"Gacc{a}",))

        def GAop(u):
            a = u % 2
            s4 = u % 4
            S.op("dve", lambda e, a=a, s4=s4: e.tensor_tensor(out=GA[a], in0=Gacc[a], in1=ga[s4], op=ALU.mult), r=(f"Gacc{a}", f"ga{s4}"), w=(f"GA{a}",))

        def stageC1(u):
            a = u % 2
            ptv = PT[:, 8 * a:8 * a + 4, :]
            for j in range(GC):
                S.op("pe", lambda e, a=a, j=j, ptv=ptv: e.transpose(out=ptv[:, j, :], in_=GA[a][:, j * P:(j + 1) * P], identity=ident[:]),
                     r=(f"GA{a}", "ident"), w=(f"PT{a}",))

        def stageC2(u):
            g, i = units[u]
            a = u % 2
            vt, vk = wts[("v", g)]
            ptv = PT[:, 8 * a:8 * a + 4, :]
            S.op("act", lambda e, a=a, ptv=ptv: e.copy(out=GAT[a], in_=ptv), r=(f"PT{a}",), w=(f"GAT{a}",))
            for dc in range(4):
                for j in range(GC):
                    S.op("pe", lambda e, a=a, j=j, dc=dc, vt=vt: e.matmul(PBt[:, dc * 512:(dc + 1) * 512], lhsT=GAT[a][:, j, :], rhs=vt[:, j, dc * 512:(dc + 1) * 512],
                                                                     start=(j == 0), stop=(j == GC - 1)),
                         r=(f"GAT{a}", vk), w=("PM", "PS", "PX"))

            def add():
                S.op("dve", lambda e, i=i: e.tensor_tensor(out=xres[:, i, :], in0=xres[:, i, :], in1=PBt[:], op=ALU.add), r=("PM", "PS", "PX", f"xres{i}"), w=(f"xres{i}",))
            return add

        def a_pe_with_loads(u):
            stageA_pe(u)
            g1, i1 = units[u]
            if i1 == NT - 1 and g1 + 1 < NG:
                load_v(g1 + 1)

        load_u(0)
        load_v(0)
        load_u(1)
        a_pe_with_loads(0)
        a_pe_with_loads(1)
        gelu_pair(0)
        a_pe_with_loads(2)
        a_pe_with_loads(3)
        LOOK = NZ - 1
        NH = 8 * NU
        for n in range(min(LOOK, NH)):
            Z(n)
        pending = None
        for n in range(NH):
            u, h = divmod(n, 8)
            if n + LOOK < NH:
                Z(n + LOOK)
            M(n)
            if h == 3 and u >= 1:
                if u % 2 == 0:
                    gelu_pair(u // 2)
                pending()
                pending = None
                gp, ip = units[u - 1]
                if ip == NT - 1 and gp + 2 < NG:
                    load_u(gp + 2)
                if u % 2 == 0:
                    for uu in (u + 2, u + 3):
                        if uu < NU:
                            a_pe_with_loads(uu)
            if h == 7:
                GAop(u)
                stageC1(u)
                pending = stageC2(u)
        pending()
        gslot = nxt("w", 3)
        gfv = Wring[gslot][:].rearrange("p k c -> p (k c)")[:, 0:2 * D].bitcast(F32)
        gfk = f"W{gslot}"
        S.op("pool", lambda e: e.dma_start(out=gfv, in_=gf_d.partition_broadcast(P)), w=(gfk,), dma=gfk)
        for i in range(NT):
            rms_rstd(xres[:, i, :], f"xres{i}", 8 + i, D)
            S.op("dve", lambda e, i=i: e.scalar_tensor_tensor(out=xres[:, i, :], in0=xres[:, i, :], scalar=rstd[:, 8 + i:9 + i], in1=gfv, op0=ALU.mult, op1=ALU.mult),
                 r=(f"xres{i}", f"rstd{8 + i}", gfk), w=(f"xres{i}",))

    def phaseOut(blk):
        b0 = (blk - nb_pre) * TB
        for i in range(NT):
            S.op("sp", lambda e, i=i: e.dma_start(out=y_d[b0 + i * P: b0 + (i + 1) * P, :], in_=xres[:, i, :]), r=(f"xres{i}",), dma=f"out{i}")

    for blk in range(nb_pre + nb_own):
        is_pre = blk < nb_pre
        phaseA(blk, is_pre, is_pre and blk == nb_pre - 1)
        if is_pre and blk == nb_pre - 1:
            S.op("dve", lambda e: e.tensor_scalar(out=Cst[:], in0=Cst[:], scalar1=flag[:, 0:1], scalar2=None, op0=ALU.mult), r=("Cst", "flag"), w=("Cst",))
            S.op("dve", lambda e: e.tensor_scalar(out=hist[:], in0=hist[:], scalar1=flag[:, 0:1], scalar2=None, op0=ALU.mult), r=("hist", "flag"), w=("hist",))
        if not is_pre:
            if stop_after != "A":
                phaseP(blk)
            phaseOut(blk)

    finals = [f"out{i}" for i in range(NT)] + (["dbg"] if dbg_out else [])
    S.emit(finals)
    es.close()
    return nc, dbg_out


def prep_weights(inp):
    f = np.float32
    c = lambda a: np.ascontiguousarray(a, dtype=f)
    w = {}
    w["g1T"] = c(inp["norm1_g"].reshape(KD, P).T)
    w["g2T"] = c(inp["norm2_g"].reshape(KD, P).T)
    w["gmixT"] = c(np.concatenate([inp["gm_out_g"], inp["ml_out_g"]]).reshape(KD, P).T)
    w["gf"] = c(inp["final_g"])
    w["gvn"] = c(inp["gm_vnorm_g"])
    w["w_in"] = c(inp["w_in"])
    w["wsT"] = c(np.transpose(inp["w_spatial"], (2, 0, 1)))
    w["bsp"] = c(inp["b_spatial"].T)
    w["cw"] = c(np.transpose(inp["ml_conv_w"].reshape(4, 8, P), (2, 1, 0)))
    w["cb"] = c(inp["ml_conv_b"].reshape(8, P).T)
    w["bi"] = c(inp["ml_b_i"])
    w["bf"] = c(inp["ml_b_f"])
    w["w_out"] = c(inp["w_out"])
    w["wq"] = c(inp["peer_wq"])
    kt = np.zeros((P, 8, 256), f)
    kt[0:64, :, 0:128] = np.transpose(inp["peer_k1"], (2, 0, 1))
    kt[64:128, :, 128:256] = np.transpose(inp["peer_k2"], (2, 0, 1))
    w["kT12"] = kt
    w["UT"] = c(np.transpose(inp["peer_u"].T.reshape(KD, P, NEXP), (1, 0, 2)))
    w["V"] = c(inp["peer_v"])
    w["ident"] = np.eye(P, dtype=f)
    w["maskT"] = np.triu(np.ones((P, P), f))
    return w


def kernel(**inputs):
    x = np.asarray(inputs["x"], dtype=np.float32)
    w = prep_weights(inputs)
    half = SEQ // 2
    nb = half // TB
    nc, _ = build(nb, nb)
    in_maps = []
    for c in range(NCORES):
        b, hf = c // 2, c % 2
        own = np.ascontiguousarray(x[b, hf * half:(hf + 1) * half])
        prev = np.ascontiguousarray(x[b, 0:half]) if hf == 1 else own
        m = dict(w)
        m["xo"] = own
        m["xp"] = prev
        m["flag"] = np.full((P, 1), float(hf), np.float32)
        in_maps.append(m)
    res = run_bass_kernel_spmd(nc, in_maps, core_ids=list(range(NCORES)))
    out = np.empty((BATCH, SEQ, D), np.float32)
    for c in range(NCORES):
        b, hf = c // 2, c % 2
        out[b, hf * half:(hf + 1) * half] = res.results[c]["y"]
    return out
```
